# Optimizing a Trainium2 kernel written in Bass

```python
import jax
import jax.numpy as jnp
from jax import lax
import numpy as np

D_MODEL = 2048
BATCH = 4
SEQ = 8192
DEPTH = 1

EPS = 1e-6
NEG = -1e30
FORCE_BONUS = 1e4
NSA_HEAD_DIM = 128
NSA_WIDTH = D_MODEL // 2
NSA_HEADS = NSA_WIDTH // NSA_HEAD_DIM
NSA_REP = 4
NSA_KV_GROUPS = NSA_HEADS // NSA_REP
NSA_KV_WIDTH = NSA_KV_GROUPS * NSA_HEAD_DIM
CMP_BLOCK = 32
CMP_STRIDE = 16
SEL_BLOCK = 64
SEL_TOPK = 16
WINDOW = 512
Q_BLOCK = 128
N_BRANCH = 3
MLSTM_HEAD_DIM = 256
MLSTM_WIDTH = D_MODEL - NSA_WIDTH
MLSTM_HEADS = MLSTM_WIDTH // MLSTM_HEAD_DIM
MLSTM_CHUNK = 64
CONV_WIDTH = 4
D_FF = 4 * D_MODEL
IN_SIZES = (NSA_WIDTH,) + (NSA_KV_WIDTH,) * 6 + (NSA_HEADS * N_BRANCH, 2 * MLSTM_WIDTH, MLSTM_WIDTH, MLSTM_WIDTH, MLSTM_HEADS, MLSTM_HEADS)
D_IN = sum(IN_SIZES)
SPLIT_IDX = tuple(int(v) for v in np.cumsum(IN_SIZES)[:-1])

kernel_name = 'hymba_nsa_mlstm_hybrid'


def _rms_norm(x, g):
    xf = x.astype(jnp.float32)
    y = xf * lax.rsqrt(jnp.mean(xf * xf, axis=-1, keepdims=True) + EPS)
    return (y * g.astype(jnp.float32)).astype(x.dtype)


def _alibi_slopes(n):
    return jnp.exp2(-8.0 * jnp.arange(1, n + 1, dtype=jnp.float32) / n)


def _causal_conv_silu(u, w, b):
    S = u.shape[1]
    up = jnp.pad(u, ((0, 0), (CONV_WIDTH - 1, 0), (0, 0)))
    y = b + up[:, 0:S] * w[0]
    for i in range(1, CONV_WIDTH):
        y = y + up[:, i:i + S] * w[i]
    return jax.nn.silu(y)


def _nsa_mixer(q, k_cmp, v_cmp, k_slc, v_slc, k_win, v_win, gate_pre,
               w_cmp_k1, w_cmp_k2, pos_cmp_k, w_cmp_v1, w_cmp_v2, pos_cmp_v):
    B, S = q.shape[0], q.shape[1]
    G, R, Dh = NSA_KV_GROUPS, NSA_REP, NSA_HEAD_DIM
    f32 = jnp.float32
    n_sub = S // CMP_STRIDE
    ratio = CMP_BLOCK // CMP_STRIDE
    n_cmp = n_sub - ratio + 1
    n_sel = S // SEL_BLOCK
    n_qb = S // Q_BLOCK
    topk = min(SEL_TOPK, n_sel)
    scale = Dh ** -0.5
    slopes = _alibi_slopes(NSA_HEADS).reshape(G, R)[None, :, :, None, None]

    def heads(t):
        return t.reshape(B, S, G, Dh).transpose(0, 2, 1, 3).astype(f32)

    def compress(t, w1, w2, pos):
        sub = heads(t).reshape(B, G, n_sub, CMP_STRIDE, Dh)
        blocks = jnp.concatenate([sub[:, :, r:r + n_cmp] for r in range(ratio)], axis=3)
        blocks = (blocks + pos.astype(f32)).reshape(B, G, n_cmp, CMP_BLOCK * Dh)
        return jax.nn.silu(blocks @ w1.astype(f32)) @ w2.astype(f32)

    kc = compress(k_cmp, w_cmp_k1, w_cmp_k2, pos_cmp_k)
    vc = compress(v_cmp, w_cmp_v1, w_cmp_v2, pos_cmp_v)
    cmp_start = jnp.arange(n_cmp) * CMP_STRIDE
    cmp_end = cmp_start + CMP_BLOCK - 1
    sel_start = jnp.arange(n_sel) * SEL_BLOCK
    overlap = ((cmp_start[:, None] < sel_start[None, :] + SEL_BLOCK) & (cmp_end[:, None] >= sel_start[None, :])).astype(f32)

    ks_blk = heads(k_slc).reshape(B, G, n_sel, SEL_BLOCK, Dh)
    vs_blk = heads(v_slc).reshape(B, G, n_sel, SEL_BLOCK, Dh)
    kw_pad = jnp.pad(heads(k_win), ((0, 0), (0, 0), (WINDOW, 0), (0, 0)))
    vw_pad = jnp.pad(heads(v_win), ((0, 0), (0, 0), (WINDOW, 0), (0, 0)))
    bi = jnp.arange(B)[:, None, None, None]
    gi = jnp.arange(G)[None, :, None, None]
    sel_ids = jnp.arange(n_sel)
    sel_off = jnp.arange(SEL_BLOCK)
    win_off = jnp.arange(WINDOW + Q_BLOCK)

    q_blocks = q.reshape(B, n_qb, Q_BLOCK, G, R, Dh).transpose(1, 0, 3, 4, 2, 5)
    g_blocks = gate_pre.reshape(B, n_qb, Q_BLOCK, G, R, N_BRANCH).transpose(1, 0, 3, 4, 2, 5)

    def block_fn(args):
        qb, gb, c = args
        qb = qb.astype(f32) * scale
        t0 = c * Q_BLOCK
        t = t0 + jnp.arange(Q_BLOCK)
        s_c = jnp.einsum('bgrqd,bgnd->bgrqn', qb, kc)
        dist_c = t[:, None] - cmp_end[None, :]
        valid_c = dist_c >= 0
        s_c = jnp.where(valid_c, s_c - slopes * dist_c.astype(f32), NEG)
        p_c = jax.nn.softmax(s_c, axis=-1) * valid_c
        o_c = jnp.einsum('bgrqn,bgnd->bgrqd', p_c, vc)
        imp = jnp.einsum('bgrqn,nj->bgqj', p_c, overlap)
        cur = t // SEL_BLOCK
        forced = (sel_ids[None, :] == 0) | (sel_ids[None, :] == cur[:, None]) | (sel_ids[None, :] == cur[:, None] - 1)
        causal_blk = sel_start[None, :] <= t[:, None]
        imp = jnp.where(causal_blk, jnp.where(forced, imp + FORCE_BONUS, imp), NEG)
        _, idx = lax.top_k(imp, topk)
        ks_g = ks_blk[bi, gi, idx].reshape(B, G, Q_BLOCK, topk * SEL_BLOCK, Dh)
        vs_g = vs_blk[bi, gi, idx].reshape(B, G, Q_BLOCK, topk * SEL_BLOCK, Dh)
        pos_s = (idx[..., None] * SEL_BLOCK + sel_off).reshape(B, G, Q_BLOCK, topk * SEL_BLOCK)
        dist_s = (t[None, None, :, None] - pos_s)[:, :, None]
        s_s = jnp.einsum('bgrqd,bgqkd->bgrqk', qb, ks_g)
        s_s = jnp.where(dist_s >= 0, s_s - slopes * dist_s.astype(f32), NEG)
        o_s = jnp.einsum('bgrqk,bgqkd->bgrqd', jax.nn.softmax(s_s, axis=-1), vs_g)
        kw = lax.dynamic_slice_in_dim(kw_pad, t0, WINDOW + Q_BLOCK, axis=2)
        vw = lax.dynamic_slice_in_dim(vw_pad, t0, WINDOW + Q_BLOCK, axis=2)
        pos_w = t0 - WINDOW + win_off
        dist_w = t[:, None] - pos_w[None, :]
        valid_w = (dist_w >= 0) & (dist_w < WINDOW) & (pos_w[None, :] >= 0)
        s_w = jnp.einsum('bgrqd,bgkd->bgrqk', qb, kw)
        s_w = jnp.where(valid_w, s_w - slopes * dist_w.astype(f32), NEG)
        o_w = jnp.einsum('bgrqk,bgkd->bgrqd', jax.nn.softmax(s_w, axis=-1), vw)
        g = jax.nn.sigmoid(gb.astype(f32))
        return g[..., 0:1] * o_c + g[..., 1:2] * o_s + g[..., 2:3] * o_w

    out = lax.map(block_fn, (q_blocks, g_blocks, jnp.arange(n_qb)))
    return out.transpose(1, 0, 4, 2, 3, 5).reshape(B, S, NSA_WIDTH)


def _mlstm_mixer(q, k, v, o_pre, i_pre, f_pre, norm_g):
    B, S = q.shape[0], q.shape[1]
    NH, DH, L = MLSTM_HEADS, MLSTM_HEAD_DIM, MLSTM_CHUNK
    nc = S // L
    f32 = jnp.float32

    def chunks(t):
        return t.astype(f32).reshape(B, nc, L, NH, DH).transpose(1, 0, 3, 2, 4)

    def gchunks(t):
        return t.astype(f32).reshape(B, nc, L, NH).transpose(1, 0, 3, 2)

    qc, kc, vc = chunks(q), chunks(k) * DH ** -0.5, chunks(v)
    li_c = gchunks(i_pre)
    lf_c = jax.nn.log_sigmoid(gchunks(f_pre))
    causal = jnp.tril(jnp.ones((L, L), dtype=bool))

    def step(carry, xs):
        C, n, m = carry
        qj, kj, vj, li, lf = xs
        b = jnp.cumsum(lf, axis=-1)
        D = jnp.where(causal, b[..., :, None] - b[..., None, :] + li[..., None, :], NEG)
        a = b + m[..., None]
        m_j = jnp.maximum(a, jnp.max(D, axis=-1))
        w_intra = jnp.exp(D - m_j[..., None])
        w_inter = jnp.exp(a - m_j)
        sc = jnp.einsum('bhjd,bhsd->bhjs', qj, kj) * w_intra
        num = w_inter[..., None] * jnp.einsum('bhjd,bhde->bhje', qj, C) + jnp.einsum('bhjs,bhse->bhje', sc, vj)
        den = w_inter * jnp.einsum('bhjd,bhd->bhj', qj, n) + jnp.sum(sc, axis=-1)
        h = num / jnp.maximum(jnp.abs(den), jnp.exp(-m_j))[..., None]
        g = b[..., -1]
        lw = g[..., None] - b + li
        m_new = jnp.maximum(g + m, jnp.max(lw, axis=-1))
        w_s = jnp.exp(lw - m_new[..., None])
        decay = jnp.exp(g + m - m_new)
        kw = w_s[..., None] * kj
        C_new = decay[..., None, None] * C + jnp.einsum('bhsd,bhse->bhde', kw, vj)
        n_new = decay[..., None] * n + jnp.sum(kw, axis=2)
        return (C_new, n_new, m_new), h

    init = (jnp.zeros((B, NH, DH, DH), f32), jnp.zeros((B, NH, DH), f32), jnp.zeros((B, NH), f32))
    _, hs = lax.scan(step, init, (qc, kc, vc, li_c, lf_c))
    h = hs.transpose(1, 0, 3, 2, 4).reshape(B, S, NH, DH)
    h = h * lax.rsqrt(jnp.mean(h * h, axis=-1, keepdims=True) + EPS) * norm_g.astype(f32).reshape(NH, DH)
    return jax.nn.sigmoid(o_pre.astype(f32)) * h.reshape(B, S, MLSTM_WIDTH)


def setup_inputs(seed: int = 0) -> dict:
    key = jax.random.key(seed)
    ks = jax.random.split(key, 20)
    f32 = jnp.float32
    L = DEPTH

    def nrm(k, shape, s):
        return jax.random.normal(k, shape, f32) * s

    cmp_in = CMP_BLOCK * NSA_HEAD_DIM
    return {
        'x': nrm(ks[0], (BATCH, SEQ, D_MODEL), 1.0),
        'norm_mix_g': 1.0 + nrm(ks[1], (L, D_MODEL), 0.02),
        'w_in': nrm(ks[2], (L, D_MODEL, D_IN), D_MODEL ** -0.5),
        'w_cmp_k1': nrm(ks[3], (L, cmp_in, NSA_HEAD_DIM), cmp_in ** -0.5),
        'w_cmp_k2': nrm(ks[4], (L, NSA_HEAD_DIM, NSA_HEAD_DIM), NSA_HEAD_DIM ** -0.5),
        'pos_cmp_k': nrm(ks[5], (L, CMP_BLOCK, NSA_HEAD_DIM), 0.02),
        'w_cmp_v1': nrm(ks[6], (L, cmp_in, NSA_HEAD_DIM), cmp_in ** -0.5),
        'w_cmp_v2': nrm(ks[7], (L, NSA_HEAD_DIM, NSA_HEAD_DIM), NSA_HEAD_DIM ** -0.5),
        'pos_cmp_v': nrm(ks[8], (L, CMP_BLOCK, NSA_HEAD_DIM), 0.02),
        'conv_w': nrm(ks[9], (L, CONV_WIDTH, 2 * MLSTM_WIDTH), CONV_WIDTH ** -0.5),
        'conv_b': nrm(ks[10], (L, 2 * MLSTM_WIDTH), 0.01),
        'b_igate': nrm(ks[11], (L, MLSTM_HEADS), 0.1),
        'b_fgate': jnp.linspace(3.0, 6.0, MLSTM_HEADS, dtype=f32)[None, :] + nrm(ks[12], (L, MLSTM_HEADS), 0.1),
        'mlstm_norm_g': 1.0 + nrm(ks[13], (L, MLSTM_WIDTH), 0.02),
        'w_out': nrm(ks[14], (L, D_MODEL, D_MODEL), D_MODEL ** -0.5),
        'norm_mlp_g': 1.0 + nrm(ks[15], (L, D_MODEL), 0.02),
        'w_mlp_in': nrm(ks[16], (L, D_MODEL, D_FF), D_MODEL ** -0.5),
        'w_mlp_out': nrm(ks[17], (L, D_FF, D_MODEL), D_FF ** -0.5),
        'norm_f_g': 1.0 + nrm(ks[18], (D_MODEL,), 0.02),
    }


def reference(x, norm_mix_g, w_in, w_cmp_k1, w_cmp_k2, pos_cmp_k, w_cmp_v1, w_cmp_v2, pos_cmp_v,
              conv_w, conv_b, b_igate, b_fgate, mlstm_norm_g, w_out, norm_mlp_g, w_mlp_in, w_mlp_out, norm_f_g):
    for l in range(DEPTH):
        h = _rms_norm(x, norm_mix_g[l])
        (q_a, k_c, v_c, k_s, v_s, k_w, v_w, g_a,
         qk_m, v_m, o_m, i_m, f_m) = jnp.split(h @ w_in[l], SPLIT_IDX, axis=-1)
        y_a = _nsa_mixer(q_a, k_c, v_c, k_s, v_s, k_w, v_w, g_a,
                         w_cmp_k1[l], w_cmp_k2[l], pos_cmp_k[l], w_cmp_v1[l], w_cmp_v2[l], pos_cmp_v[l])
        q_m, k_m = jnp.split(_causal_conv_silu(qk_m, conv_w[l], conv_b[l]), 2, axis=-1)
        y_m = _mlstm_mixer(q_m, k_m, v_m, o_m, i_m + b_igate[l], f_m + b_fgate[l], mlstm_norm_g[l])
        mixed = jnp.concatenate([y_a.astype(x.dtype), y_m.astype(x.dtype)], axis=-1)
        x = x + mixed @ w_out[l]
        h = _rms_norm(x, norm_mlp_g[l])
        x = x + jnp.square(jax.nn.relu(h @ w_mlp_in[l])) @ w_mlp_out[l]
    return _rms_norm(x, norm_f_g)
```

```python
import contextlib
import numpy as np
import concourse.bass as bass
import concourse.mybir as mybir
from concourse.bass_utils import run_bass_kernel_spmd

F32 = mybir.dt.float32
BF16 = mybir.dt.bfloat16
ALU = mybir.AluOpType
AF = mybir.ActivationFunctionType
AX = mybir.AxisListType

D = 2048
DIN = 6688
DFF = 8192
EPS = 1e-6
NEG = -1e30
BIGM = 32768.0
EPOCH = 30000


class Buf:
    __slots__ = ("name", "w", "r", "sem", "semv", "accum", "wset", "excl")

    def __init__(self, name, accum=False, excl=False):
        self.excl = excl
        self.name = name
        self.w = None
        self.r = []
        self.sem = None
        self.semv = 0
        self.accum = accum
        self.wset = {}


class Prog:
    ENG = ("pe", "act", "dve", "pool", "sp")

    def __init__(self, nc, stack):
        self.nc = nc
        self.stack = stack
        self.streams = {e: [] for e in self.ENG}
        self.cnt = {e: 0 for e in self.ENG}
        self.sems = []
        self.esem = {e: self._newsem("e_" + e) for e in self.ENG}
        self.seen = {e: {} for e in self.ENG}
        self.out_toks = []
        self.pending = {}

    def _newsem(self, name):
        s = self.stack.enter_context(self.nc.semaphore(name + "_%d" % len(self.sems)))
        self.sems.append(s)
        return len(self.sems) - 1

    def _need(self, eng, tok, waits):
        if tok is None:
            return
        si, v = tok
        if self.seen[eng].get(si, 0) >= v:
            return
        if waits.get(si, 0) < v:
            waits[si] = v

    def _deps(self, eng, r, w):
        waits = {}
        for b in r:
            self._need(eng, b.w, waits)
            for si, v in b.wset.items():
                self._need(eng, (si, v), waits)
        for b in w:
            if not b.accum:
                self._need(eng, b.w, waits)
            for t in b.r:
                self._need(eng, t, waits)
        if eng == "pe":
            waits.pop(self.esem["pe"], None)
        for si, v in waits.items():
            self.seen[eng][si] = v
        return list(waits.items())

    def op(self, eng, fn, r=(), w=()):
        if any(b.excl for b in r):
            w = list(w) + [b for b in r if b.excl and b not in w]
            r = [b for b in r if not b.excl]
        waits = self._deps(eng, r, w)
        if self.cnt[eng] >= EPOCH:
            self.esem[eng] = self._newsem("e_" + eng)
            self.cnt[eng] = 0
        self.cnt[eng] += 1
        tok = (self.esem[eng], self.cnt[eng])
        sems = self.sems

        def emit(e, waits=waits, fn=fn, tok=tok):
            for si, v in waits:
                e.wait_ge(sems[si], v)
            fn(e).then_inc(sems[tok[0]], 1)

        self.streams[eng].append(emit)
        for b in w:
            b.w = tok
            b.r = []
        for b in r:
            b.r.append(tok)
        return tok

    def dma(self, q, out, in_, r=(), w=(), is_out=False, **kw):
        dst = w[0]
        waits = self._deps(q, r, w)
        own = dst
        if dst.accum and len(r) > 0 and not r[0].accum:
            own = r[0]
        if own.sem is None:
            own.sem = self._newsem("d_" + own.name)
        own.semv += 16
        tok = (own.sem, own.semv)
        sems = self.sems

        def emit(e, waits=waits, tok=tok, out=out, in_=in_, kw=kw):
            for si, v in waits:
                e.wait_ge(sems[si], v)
            o_ap = out(e) if callable(out) else out
            i_ap = in_(e) if callable(in_) else in_
            e.dma_start(out=o_ap, in_=i_ap, **kw).then_inc(sems[tok[0]], 16)

        self.streams[q].append(emit)
        self.pending[tok[0]] = tok[1]
        for d_ in w:
            if d_.accum:
                d_.wset[tok[0]] = tok[1]
            else:
                d_.w = tok
                d_.r = []
        for b in r:
            b.r.append(tok)
        if is_out:
            self.out_toks.append(tok)
        return tok

    def barrier(self):
        toks = dict(self.pending)
        for e in self.ENG:
            if self.cnt[e] > 0:
                toks[self.esem[e]] = self.cnt[e]
        self.pending = {}
        sems = self.sems
        for eng in self.ENG:
            waits = [(si, v) for si, v in toks.items() if self.seen[eng].get(si, 0) < v
                     and not (eng != "sp" and si == self.esem[eng])]
            for si, v in waits:
                self.seen[eng][si] = v

            def emit(e, waits=waits):
                for si, v in waits:
                    e.wait_ge(sems[si], v)

            self.streams[eng].append(emit)

    def finish(self):
        last = {}
        for si, v in self.out_toks:
            last[si] = max(last.get(si, 0), v)
        sems = self.sems

        def fin(e):
            for si, v in last.items():
                e.wait_ge(sems[si], v)

        self.streams["pool"].append(fin)
        with self.nc.Block() as block:
            @block.tensor
            def _(e):
                for f in self.streams["pe"]:
                    f(e)

            @block.scalar
            def _(e):
                for f in self.streams["act"]:
                    f(e)

            @block.vector
            def _(e):
                for f in self.streams["dve"]:
                    f(e)

            @block.gpsimd
            def _(e):
                for f in self.streams["pool"]:
                    f(e)

            @block.sync
            def _(e):
                for f in self.streams["sp"]:
                    f(e)


_REGS = {}


def freg(e, val):
    key = (id(e), float(val))
    if key not in _REGS:
        _REGS[key] = e.to_reg(float(val))
    return _REGS[key]


class Rot:
    def __init__(self, items):
        self.items = items
        self.i = 0

    def next(self):
        it = self.items[self.i % len(self.items)]
        self.i += 1
        return it


def build(S, debug=False, stages="ABCDE"):
    _REGS.clear()
    NT = S // 128
    SO = S // 2
    NCMP = S // 16 - 1
    NCT = (NCMP + 127) // 128
    NCH = S // 64
    TCA = min(2048, S)
    nc = bass.Bass("TRN2", target_bir_lowering=False)

    def din(name, shape, dt=F32):
        return nc.dram_tensor(name, list(shape), dt, kind="ExternalInput").ap()

    x = din("x", [S, D])
    x_own = din("x_own", [S // 2, D])
    hsel = din("hsel", [1, 1])
    norm_mix_g = din("norm_mix_g", [1, D])
    w_in = din("w_in", [D, DIN])
    w_cmp_k1 = din("w_cmp_k1", [4096, 128])
    w_cmp_k2 = din("w_cmp_k2", [128, 128])
    pos_cmp_k = din("pos_cmp_k", [32, 128])
    w_cmp_v1 = din("w_cmp_v1", [4096, 128])
    w_cmp_v2 = din("w_cmp_v2", [128, 128])
    pos_cmp_v = din("pos_cmp_v", [32, 128])
    conv_w = din("conv_w", [4, D])
    conv_b = din("conv_b", [1, D])
    b_igate = din("b_igate", [4, 1])
    b_fgate = din("b_fgate", [4, 1])
    mlstm_norm_g = din("mlstm_norm_g", [1, 1024])
    w_out = din("w_out", [D, D])
    norm_mlp_g = din("norm_mlp_g", [1, D])
    w_mlp_in = din("w_mlp_in", [D, DFF])
    w_mlp_out = din("w_mlp_out", [DFF, D])
    norm_f_g = din("norm_f_g", [1, D])
    y = nc.dram_tensor("y", [SO, D], F32, kind="ExternalOutput").ap()

    def dscr(name, shape, dt, out=False):
        if out or (debug and name in debug):
            return nc.dram_tensor(name, list(shape), dt, kind="ExternalOutput").ap()
        return nc.dram_tensor(name, list(shape), dt).ap()

    w_in_b = dscr("w_in_b", [D, DIN], BF16)
    w_out_b = dscr("w_out_b", [D, D], BF16)
    w1_b = dscr("w1_b", [D, DFF], BF16)
    w2_b = dscr("w2_b", [DFF, D], BF16)
    QT = dscr("QT", [8, 128, S // 2], BF16)
    KCV = dscr("KCV", [4, 128, S], BF16)
    KST = dscr("KST", [2, 128, S], BF16)
    KWT = dscr("KWT", [2, 128, S], BF16)
    VS = dscr("VS", [S, 256], BF16)
    VWG = dscr("VWG", [S, 256], F32)
    GOWN = dscr("GOWN", [S // 2, 24], F32)
    QKM = dscr("QKM", [16, 128, S], BF16)
    QKC = dscr("QKC", [16, 128, S], BF16)
    VM = dscr("VM", [S, 1024], BF16)
    OM = dscr("OM", [S // 2, 1024], F32)
    IFT = dscr("IFT", [8, S], F32)
    MIXA = dscr("MIXA", [S // 2, 1024], BF16)
    YMP = dscr("YMP", [S, 1024], BF16)

    stack = contextlib.ExitStack()
    with stack:
        P = Prog(nc, stack)

        def sb(name, shape, dt):
            return stack.enter_context(nc.sbuf_tensor(name, list(shape), dt))

        def ps(name, shape, dt=F32):
            return stack.enter_context(nc.psum_tensor(name, list(shape), dt))

        B_wib = Buf("wib", accum=True)
        B_wob = Buf("wob", accum=True)
        B_w1b = Buf("w1b", accum=True)
        B_w2b = Buf("w2b", accum=True)
        B_QT = Buf("QT", True); B_KCV = Buf("KCV", True); B_KST = Buf("KST", True); B_KWT = Buf("KWT", True)
        B_VS = Buf("VS", True); B_VWG = Buf("VWG", True); B_QKM = Buf("QKM", True); B_QKC = Buf("QKC", True)
        B_VM = Buf("VM", True); B_OM = Buf("OM", True); B_IFT = Buf("IFT", True); B_MIX = Buf("MIX", True); B_MIXA = Buf("MIXA", True); B_YMP = Buf("YMP", True); B_GOWN = Buf("GOWN", True)
        B_Y = Buf("Y", True)

        def cast_w(src, dst, buf, rows, cols, rstep):
            for r0 in range(0, rows, rstep):
                P.dma("pool", dst[r0:r0 + rstep, :], src[r0:r0 + rstep, :], w=[buf])

        ident_b = sb("ident_b", [128, 128], BF16)
        ident_f = sb("ident_f", [128, 128], F32)
        ones_f = sb("ones_f", [128, 128], F32)
        eps_t = sb("eps_t", [128, 1], F32)
        one_t = sb("one_t", [128, 1], F32)
        B_const = Buf("const")

        P.op("pool", lambda e: e.memset(ones_f[:], 1.0), w=[B_const])
        P.op("pool", lambda e: e.memset(eps_t[:], EPS), w=[B_const])
        P.op("pool", lambda e: e.memset(one_t[:], 1.0), w=[B_const])
        P.op("pool", lambda e: e.affine_select(out=ident_f[:], in_=ones_f[:], pattern=[[-1, 128]],
                                               compare_op=ALU.is_equal, fill=freg(e, 0.0), base=0, channel_multiplier=1),
             r=[B_const], w=[B_const])
        P.op("pool", lambda e: e.tensor_copy(out=ident_b[:], in_=ident_f[:]), r=[B_const], w=[B_const])

        gmix = sb("gmix", [128, D], F32)
        B_gmix = Buf("gmix")
        def bcast(ap_row, n=128):
            return ap_row.partition_broadcast(n).rearrange("p o d -> p (o d)")

        P.dma("sp", gmix[:], bcast(norm_mix_g[0:1, :]), w=[B_gmix])

        WIN_GROUPS = [(0, 512), (512, 512), (1024, 512), (1536, 256), (2048, 256), (2584, 512), (3096, 512), (3608, 512),
                      (4120, 512), (6680, 8), (1792, 256), (2304, 256), (2560, 24), (4632, 512), (5144, 512), (5656, 512), (6168, 512)]
        B_wig = {}
        for (c0g, ncg) in WIN_GROUPS:
            B_wig[c0g] = Buf("wib%d" % c0g, accum=True)
            P.dma("pool", w_in_b[:, c0g:c0g + ncg], w_in[:, c0g:c0g + ncg], w=[B_wig[c0g]])

        PS = [ps("psb%d" % i, [128, 512], F32) for i in range(8)]
        B_PS = [Buf("ps%d" % i, excl=True) for i in range(8)]


        def rms_to_bf16(xt_ap, B_x, g_tile, B_g, hb_ap, B_hb, junk_ap, B_junk, ss_ap, B_ss):
            P.op("act", lambda e: e.activation(out=junk_ap, in_=xt_ap, func=AF.Square, accum_out=ss_ap[:, 0:1]),
                 r=[B_x], w=[B_junk, B_ss])
            P.op("act", lambda e: e.activation(out=ss_ap[:, 1:2], in_=ss_ap[:, 0:1], func=AF.Sqrt,
                                               bias=eps_t[:, 0:1], scale=1.0 / D),
                 r=[B_ss, B_const], w=[B_ss])
            P.op("dve", lambda e: e.reciprocal(out=ss_ap[:, 2:3], in_=ss_ap[:, 1:2]), r=[B_ss], w=[B_ss])
            P.op("dve", lambda e: e.scalar_tensor_tensor(out=hb_ap, in0=xt_ap, scalar=ss_ap[:, 2:3], in1=g_tile,
                                                         op0=ALU.mult, op1=ALU.mult),
                 r=[B_x, B_ss, B_g], w=[B_hb])

        tr_rot = [0]

        def transpose_to(hb_ap, B_hb, dst_fn, B_dst, nblk=16):
            for f0 in range(0, nblk, 4):
                bi = 6 + (tr_rot[0] % 2)
                tr_rot[0] += 1
                tp = PS[bi][:].bitcast(BF16)
                tpv = tp[:, 0:512].rearrange("p (a b) -> p a b", a=4)

                def tfn(e, f0=f0, tpv=tpv):
                    ins = None
                    for a in range(4):
                        ins = e.transpose(out=tpv[:, a, :], in_=hb_ap[:, (f0 + a) * 128:(f0 + a + 1) * 128],
                                          identity=ident_b[:])
                    return ins

                P.op("pe", tfn, r=(B_hb if isinstance(B_hb, list) else [B_hb]) + [B_const], w=[B_PS[bi]])
                eng = "act" if (tr_rot[0] % 2) else "dve"
                dst = dst_fn(f0, 4)
                if eng == "act":
                    P.op("act", lambda e, dst=dst, tpv=tpv: e.copy(out=dst, in_=tpv), r=[B_PS[bi]], w=[B_dst])
                else:
                    P.op("dve", lambda e, dst=dst, tpv=tpv: e.tensor_copy(out=dst, in_=tpv), r=[B_PS[bi]], w=[B_dst])

        FM_ALL = [
            (1024, 512, "kcv"), (1536, 256, "ks"), (2048, 256, "kw"),
            (2584, 512, "qkm0"), (3096, 512, "qkm1"), (3608, 512, "qkm2"), (4120, 512, "qkm3"), (6680, 8, "if"),
        ]
        TM_ALL = [(1792, 256, "vs"), (2304, 256, "vwg"), (4632, 512, "vm0"), (5144, 512, "vm1")]
        FM_OWN = [(0, 512, "q0"), (512, 512, "q1")]
        TM_OWN = [(2560, 24, "go"), (5656, 512, "om0"), (6168, 512, "om1")]
        with contextlib.ExitStack() as st_a:
            def sba(name, shape, dt):
                return st_a.enter_context(nc.sbuf_tensor(name, list(shape), dt))

            xa = [sba("xa%d" % i, [128, D], F32) for i in range(4)]
            B_xa = [Buf("xa%d" % i) for i in range(4)]
            hb = [sba("hb%d" % i, [128, D], BF16) for i in range(4)]
            B_hb = [Buf("hb%d" % i) for i in range(4)]
            junk = sba("junkA", [128, D], BF16)
            B_junk = Buf("junkA")
            ssA = [sba("ssA%d" % i, [128, 4], F32) for i in range(4)]
            B_ssA = [Buf("ssA%d" % i) for i in range(4)]
            hTs = [sba("hT%d" % i, [128, 16, 1024], BF16) for i in range(2)]
            B_hTs = [Buf("hT%d" % i) for i in range(2)]
            wg = [sba("wg%d" % i, [128, 16, 512], BF16) for i in range(2)]
            B_wg = [Buf("wg%d" % i) for i in range(2)]
            stg = [sba("stg%d" % i, [128, 2048], F32) for i in range(2)]
            B_stg = [Buf("stg%d" % i) for i in range(2)]
            ev = [0]
            wgi = [0]
            sti = [0]

            def evac(dst_ap, src_ap, B_src, B_dst, scale=None):
                ev[0] += 1
                if scale is not None:
                    P.op("act", lambda e: e.activation(out=dst_ap, in_=src_ap, func=AF.Copy, scale=scale),
                         r=[B_src], w=[B_dst])
                elif ev[0] % 2:
                    P.op("act", lambda e: e.copy(out=dst_ap, in_=src_ap), r=[B_src], w=[B_dst])
                else:
                    P.op("dve", lambda e: e.tensor_copy(out=dst_ap, in_=src_ap), r=[B_src], w=[B_dst])

            TCA = 1024
            xai = [0]
            bki = [0]

            def norm_tile(xsrc, c0, t, hT, B_hT):
                i = xai[0] % 4
                xai[0] += 1
                r0 = c0 + t * 128
                P.dma("sp", xa[i][:], xsrc[r0:r0 + 128, :], w=[B_xa[i]])
                rms_to_bf16(xa[i][:], B_xa[i], gmix[:], B_gmix, hb[i][:], B_hb[i], junk[:], B_junk,
                            ssA[i], B_ssA[i])
                transpose_to(hb[i], B_hb[i],
                             lambda f0, n, t=t: hT[:, f0:f0 + n, t * 128:(t + 1) * 128], B_hT)

            def proj_chunk(c0, FM_GROUPS, TM_GROUPS, hT, B_hT, after_group):
                if True:
                    for (col0, ncols, kind) in FM_GROUPS:
                        wi = wgi[0] % 2
                        wgi[0] += 1
                        P.dma("sp", wg[wi][:, :, 0:ncols],
                              w_in_b[:, col0:col0 + ncols].rearrange("(f p) c -> p f c", p=128),
                              r=[B_wig[col0]], w=[B_wg[wi]])
                        for m0 in range(0, ncols, 128):
                            m = min(128, ncols - m0)
                            si = sti[0] % 2
                            sti[0] += 1
                            is_f32 = (kind == "if")
                            sview = stg[si][:] if is_f32 else stg[si][:].bitcast(BF16)
                            bki[0] += 2
                            for s0 in range(0, TCA, 512):
                                bi = (bki[0] + s0 // 512) % 4

                                def mm(e, wi=wi, m0=m0, m=m, s0=s0, bi=bi):
                                    ins = None
                                    for fc in range(16):
                                        ins = e.matmul(PS[bi][0:m, :], lhsT=wg[wi][:, fc, m0:m0 + m],
                                                       rhs=hT[:, fc, s0:s0 + 512], start=(fc == 0), stop=(fc == 15))
                                    return ins

                                P.op("pe", mm, r=[B_wg[wi], B_hT], w=[B_PS[bi]])
                                evac(sview[0:m, s0:s0 + 512], PS[bi][0:m, :], B_PS[bi], B_stg[si],
                                     scale=(128 ** -0.5) if kind in ("q0", "q1") else None)
                            blk = (col0 + m0)
                            if kind in ("q0", "q1"):
                                dst, Bd = QT[blk // 128, :, c0:c0 + TCA], B_QT
                            elif kind == "kcv":
                                dst, Bd = KCV[(blk - 1024) // 128, :, c0:c0 + TCA], B_KCV
                            elif kind == "ks":
                                dst, Bd = KST[(blk - 1536) // 128, :, c0:c0 + TCA], B_KST
                            elif kind == "kw":
                                dst, Bd = KWT[(blk - 2048) // 128, :, c0:c0 + TCA], B_KWT
                            elif kind == "if":
                                dst, Bd = IFT[:, c0:c0 + TCA], B_IFT
                            else:
                                dst, Bd = QKM[(blk - 2584) // 128, :, c0:c0 + TCA], B_QKM
                            P.dma("pool", dst, sview[0:m, 0:TCA], r=[B_stg[si]], w=[Bd])
                        after_group()
                    for (col0, ncols, kind) in TM_GROUPS:
                        wi = wgi[0] % 2
                        wgi[0] += 1
                        P.dma("sp", wg[wi][:, :, 0:ncols],
                              w_in_b[:, col0:col0 + ncols].rearrange("(f p) c -> p f c", p=128),
                              r=[B_wig[col0]], w=[B_wg[wi]])
                        is_f32 = kind in ("vwg", "om0", "om1", "go")
                        ntile = TCA // 128
                        for t0 in range(0, ntile, 4):
                            si = sti[0] % 2
                            sti[0] += 1
                            sview = (stg[si][:] if is_f32 else stg[si][:].bitcast(BF16))[:, 0:4 * ncols].rearrange(
                                "p (t c) -> p t c", t=4)
                            for tt in range(4):
                                t = t0 + tt
                                bi = tt % 4

                                def mm(e, wi=wi, t=t, bi=bi, ncols=ncols):
                                    ins = None
                                    for fc in range(16):
                                        ins = e.matmul(PS[bi][:, 0:ncols], lhsT=hT[:, fc, t * 128:(t + 1) * 128],
                                                       rhs=wg[wi][:, fc, 0:ncols], start=(fc == 0), stop=(fc == 15))
                                    return ins

                                P.op("pe", mm, r=[B_wg[wi], B_hT], w=[B_PS[bi]])
                                evac(sview[:, tt, :], PS[bi][:, 0:ncols], B_PS[bi], B_stg[si])
                            r0 = c0 + t0 * 128
                            if kind == "vs":
                                dst, Bd = VS[r0:r0 + 512, :], B_VS
                            elif kind == "vwg":
                                dst, Bd = VWG[r0:r0 + 512, :], B_VWG
                            elif kind == "go":
                                dst, Bd = GOWN[r0:r0 + 512, :], B_GOWN
                            elif kind in ("vm0", "vm1"):
                                o = 0 if kind == "vm0" else 512
                                dst, Bd = VM[r0:r0 + 512, o:o + 512], B_VM
                            else:
                                o = 0 if kind == "om0" else 512
                                dst, Bd = OM[r0:r0 + 512, o:o + 512], B_OM
                            P.dma("pool", dst.rearrange("(t p) c -> p t c", p=128), sview, r=[B_stg[si]], w=[Bd])
                        after_group()

            chunks = [(x, c0, FM_ALL, TM_ALL) for c0 in range(0, S, TCA)] + \
                     [(x_own, c0, FM_OWN, TM_OWN) for c0 in range(0, SO, TCA)]
            if "A" not in stages:
                chunks = []
            for t in range(TCA // 128 if chunks else 0):
                norm_tile(chunks[0][0], chunks[0][1], t, hTs[0], B_hTs[0])
            for k, (xsrc, c0, FMG, TMG) in enumerate(chunks):
                pend = []
                if k + 1 < len(chunks):
                    nx = chunks[k + 1]
                    pend = [(nx[0], nx[1], t, hTs[(k + 1) % 2], B_hTs[(k + 1) % 2]) for t in range(TCA // 128)]
                ng = len(FMG) + len(TMG)
                per = -(-len(pend) // ng) if pend else 0

                def after_group(pend=pend, per=per):
                    for _ in range(per):
                        if pend:
                            norm_tile(*pend.pop(0))

                proj_chunk(c0, FMG, TMG, hTs[k % 2], B_hTs[k % 2], after_group)
                while pend:
                    norm_tile(*pend.pop(0))

        P.barrier()
        cast_w(w_out, w_out_b, B_wob, D, D, 256)
        cast_w(w_mlp_in, w1_b, B_w1b, D, DFF, 128)
        cast_w(w_mlp_out, w2_b, B_w2b, DFF, D, 512)

        if "D" in stages:
          with contextlib.ExitStack() as st_d0:
            def sbd0(name, shape, dt):
                return st_d0.enter_context(nc.sbuf_tensor(name, list(shape), dt))
            TP = min(2048, S)
            prm = sbd0("prm", [5, D], F32); B_prm = Buf("prm")
            CW = sbd0("CW", [128, 16, 5], F32); B_CW = Buf("CW")
            U = [sbd0("U%d" % i, [128, 3 + TP], BF16) for i in range(2)]; B_U = [Buf("U%d" % i) for i in range(2)]
            Yc = [sbd0("Yc%d" % i, [128, TP], F32) for i in range(2)]; B_Yc = [Buf("Yc%d" % i) for i in range(2)]
            Z = [sbd0("Z%d" % i, [128, TP], BF16) for i in range(2)]; B_Z = [Buf("Z%d" % i) for i in range(2)]
            P.dma("sp", prm[0:4, :], conv_w, w=[B_prm])
            P.dma("sp", prm[4:5, :], conv_b, w=[B_prm])
            for b in range(16):
                P.op("pe", lambda e, b=b: e.transpose(out=PS[5][:, 0:5], in_=prm[0:5, b * 128:(b + 1) * 128],
                                                      identity=ident_f[0:5, 0:5]), r=[B_prm, B_const], w=[B_PS[5]])
                P.op("dve", lambda e, b=b: e.tensor_copy(out=CW[:, b, :], in_=PS[5][:, 0:5]), r=[B_PS[5]], w=[B_CW])
            it = 0
            for b in range(16):
                for p0 in range(0, S, TP):
                    i = it % 2
                    it += 1
                    if p0 == 0:
                        P.op("dve", lambda e, i=i: e.memset(U[i][:, 0:3], 0.0), w=[B_U[i]])
                        P.dma("sp", U[i][:, 3:3 + TP], QKM[b, :, 0:TP], r=[B_QKM], w=[B_U[i]])
                    else:
                        P.dma("sp", U[i][:, 0:3 + TP], QKM[b, :, p0 - 3:p0 + TP], r=[B_QKM], w=[B_U[i]])
                    P.op("dve", lambda e, i=i, b=b: e.tensor_scalar(out=Yc[i][:], in0=U[i][:, 3:3 + TP],
                                                                    scalar1=CW[:, b, 3:4], scalar2=CW[:, b, 4:5],
                                                                    op0=ALU.mult, op1=ALU.add),
                         r=[B_U[i], B_CW], w=[B_Yc[i]])
                    for tap in range(3):
                        P.op("dve", lambda e, i=i, b=b, tap=tap: e.scalar_tensor_tensor(
                            out=Yc[i][:], in0=U[i][:, tap:tap + TP], scalar=CW[:, b, tap:tap + 1], in1=Yc[i][:],
                            op0=ALU.mult, op1=ALU.add), r=[B_U[i], B_CW, B_Yc[i]], w=[B_Yc[i]])
                    P.op("act", lambda e, i=i: e.activation(out=Z[i][:], in_=Yc[i][:], func=AF.Silu),
                         r=[B_Yc[i]], w=[B_Z[i]])
                    if b >= 8:
                        P.op("pool", lambda e, i=i: e.tensor_scalar(out=Z[i][:], in0=Z[i][:], scalar1=0.0625,
                                                                    scalar2=1.0, op0=ALU.mult, op1=ALU.mult),
                             r=[B_Z[i]], w=[B_Z[i]])
                    P.dma("pool", QKC[b, :, p0:p0 + TP], Z[i][:], r=[B_Z[i]], w=[B_QKC])

          P.barrier()
          with contextlib.ExitStack() as st_d:
            def sbd(name, shape, dt):
                return st_d.enter_context(nc.sbuf_tensor(name, list(shape), dt))
            PCH = 16
            PW = PCH * 64
            NPC = NCH // PCH
            gi = sbd("gi", [4, PCH, 64], F32); gf = sbd("gf", [4, PCH, 64], F32)
            nbt = sbd("nbt", [4, PCH, 64], F32)
            Ug = [sbd("Ug%d" % i, [4, PW], F32) for i in range(2)]
            CMg = [sbd("CMg%d" % i, [4, PW], F32) for i in range(2)]
            WIg = [sbd("WIg%d" % i, [4, PW], F32) for i in range(2)]
            WSg = [sbd("WSg%d" % i, [4, PW], F32) for i in range(2)]
            FLg = [sbd("FLg%d" % i, [4, PW], F32) for i in range(2)]
            DECB = [sbd("DECB%d" % i, [128, 4, PCH], F32) for i in range(2)]
            B_gp = [Buf("gp%d" % i) for i in range(2)]
            B_DECB = [Buf("DECB%d" % i) for i in range(2)]
            marr = sbd("marr", [4, NCH + 1], F32); ncml = sbd("ncml", [4, NCH], F32); decay = sbd("decay", [4, NCH], F32)
            bI = sbd("bI", [4, 1], F32); bF = sbd("bF", [4, 1], F32)
            tmpg = sbd("tmpg", [4, PCH, 64], F32)
            OH = sbd("OH", [4, 4, 128], F32)
            B_g = Buf("gates")
            B_gi = Buf("gi"); B_gf = Buf("gf")
            B_OH = Buf("OH")
            P.dma("sp", bI[:], b_igate, w=[B_g])
            P.dma("sp", bF[:], b_fgate, w=[B_g])
            P.op("dve", lambda e: e.tensor_scalar(out=bF[:], in0=bF[:], scalar1=-1.0, scalar2=None, op0=ALU.mult),
                 r=[B_g], w=[B_g])
            P.op("dve", lambda e: e.memset(marr[:, 0:1], 0.0), r=[B_g], w=[B_g])
            P.op("pool", lambda e: e.memset(OH[:], 1.0), w=[B_OH])
            P.op("pool", lambda e: e.affine_select(out=OH[:], in_=OH[:], pattern=[[-1, 4], [0, 128]],
                                                   compare_op=ALU.is_equal, fill=freg(e, 0.0), base=0, channel_multiplier=1),
                 r=[B_OH], w=[B_OH])

            def gates(pc):
                pi = pc % 2
                t0 = pc * PW
                U_, CM_, WI_, WS_, FL_ = Ug[pi], CMg[pi], WIg[pi], WSg[pi], FLg[pi]
                dv = lambda fn: P.op("dve", fn, r=[B_g, B_gp[pi]], w=[B_g, B_gp[pi]])
                ac = lambda fn: P.op("act", fn, r=[B_g, B_gp[pi], B_const], w=[B_g, B_gp[pi]])
                P.dma("sp", gi[:].rearrange("p a b -> p (a b)"), IFT[0:4, t0:t0 + PW], r=[B_IFT, B_g], w=[B_gi])
                P.dma("sp", gf[:].rearrange("p a b -> p (a b)"), IFT[4:8, t0:t0 + PW], r=[B_IFT, B_g], w=[B_gf])
                P.op("act", lambda e: e.activation(out=gf[:], in_=gf[:], func=AF.Exp, bias=bF[:, 0:1], scale=-1.0),
                     r=[B_gf, B_g], w=[B_gf])
                P.op("act", lambda e: e.activation(out=gf[:], in_=gf[:], func=AF.Ln, bias=one_t[0:4, 0:1], scale=1.0),
                     r=[B_gf, B_const], w=[B_gf])
                for n in range(PCH):
                    P.op("dve", lambda e, n=n: e.tensor_tensor_scan(out=nbt[:, n, :], data0=ones_f[0:4, 0:64],
                                                                    data1=gf[:, n, :], initial=0.0, op0=ALU.mult,
                                                                    op1=ALU.add), r=[B_gf, B_const, B_g], w=[B_g])
                P.op("dve", lambda e: e.scalar_tensor_tensor(
                    out=U_[:, 0:PW], in0=gi[:].rearrange("p a b -> p (a b)"), scalar=bI[:, 0:1],
                    in1=nbt[:].rearrange("p a b -> p (a b)"), op0=ALU.add, op1=ALU.add),
                    r=[B_gi, B_g, B_gp[pi]], w=[B_g, B_gp[pi]])
                for n in range(PCH):
                    cn = pc * PCH + n
                    c0_ = n * 64
                    dv(lambda e, c0_=c0_, cn=cn: e.tensor_tensor_scan(
                        out=CM_[:, c0_:c0_ + 64], data0=U_[:, c0_:c0_ + 64], data1=U_[:, c0_:c0_ + 64],
                        initial=marr[:, cn:cn + 1], op0=ALU.max, op1=ALU.max))
                    dv(lambda e, c0_=c0_, cn=cn, n=n: e.tensor_tensor(
                        out=marr[:, cn + 1:cn + 2], in0=CM_[:, c0_ + 63:c0_ + 64], in1=nbt[:, n, 63:64],
                        op=ALU.subtract))
                    ac(lambda e, c0_=c0_, cn=cn: e.activation(out=WI_[:, c0_:c0_ + 64], in_=CM_[:, c0_:c0_ + 64],
                                                              func=AF.Exp, bias=marr[:, cn:cn + 1], scale=-1.0))
                dv(lambda e: e.tensor_scalar(out=ncml[:, pc * PCH:(pc + 1) * PCH], in0=CM_[:, 63:PW:64],
                                             scalar1=-1.0, scalar2=None, op0=ALU.mult))
                for n in range(PCH):
                    cn = pc * PCH + n
                    c0_ = n * 64
                    ac(lambda e, c0_=c0_, cn=cn: e.activation(out=WS_[:, c0_:c0_ + 64], in_=U_[:, c0_:c0_ + 64],
                                                              func=AF.Exp, bias=ncml[:, cn:cn + 1], scale=1.0))
                dv(lambda e: e.tensor_tensor(out=decay[:, pc * PCH:(pc + 1) * PCH],
                                             in0=marr[:, pc * PCH:(pc + 1) * PCH],
                                             in1=ncml[:, pc * PCH:(pc + 1) * PCH], op=ALU.add))
                ac(lambda e: e.activation(out=decay[:, pc * PCH:(pc + 1) * PCH],
                                          in_=decay[:, pc * PCH:(pc + 1) * PCH], func=AF.Exp))
                dv(lambda e: e.tensor_tensor(out=tmpg[:].rearrange("p a b -> p (a b)"),
                                             in0=nbt[:].rearrange("p a b -> p (a b)"), in1=CM_[:, 0:PW],
                                             op=ALU.subtract))
                ac(lambda e: e.activation(out=FL_[:, 0:PW], in_=tmpg[:].rearrange("p a b -> p (a b)"), func=AF.Exp))
                dv(lambda e: e.tensor_scalar(out=CM_[:, 0:PW], in0=CM_[:, 0:PW], scalar1=-1.0, scalar2=None,
                                             op0=ALU.mult))
                for hd in range(4):
                    P.op("pe", lambda e, hd=hd: e.matmul(PS[5][:, 0:PCH], lhsT=OH[:, hd, :],
                                                         rhs=decay[:, pc * PCH:(pc + 1) * PCH], start=True, stop=True),
                         r=[B_OH, B_g], w=[B_PS[5]])
                    P.op("dve", lambda e, hd=hd: e.tensor_copy(out=DECB[pi][:, hd, :], in_=PS[5][:, 0:PCH]),
                         r=[B_PS[5]], w=[B_DECB[pi]])

            qk = [sbd("qk%d" % i, [128, 16, 64], BF16) for i in range(2)]; B_qk = [Buf("qk%d" % i) for i in range(2)]
            vt = [sbd("vt%d" % i, [64, 4, 257], BF16) for i in range(2)]; B_vt = [Buf("vt%d" % i) for i in range(2)]
            gml = sbd("gml", [64, 1024], F32); B_gml = Buf("gml")
            Cst = sbd("Cst", [128, 8, 256], F32); B_C = [Buf("Cst%d" % h) for h in range(4)]
            Cb = sbd("Cb", [128, 8, 256], BF16); B_Cb = [Buf("Cb%d" % h) for h in range(4)]
            nvec = sbd("nvec", [128, 8], F32); B_nv = Buf("nvec")
            nbv = sbd("nbv", [128, 8], BF16); B_nbv = Buf("nbv")
            GT = [sbd("GT%d" % i, [64, 12], F32) for i in range(2)]; B_GT = [Buf("GT%d" % i) for i in range(2)]
            Am = [sbd("Am%d" % i, [64, 4, 64], F32) for i in range(2)]; B_Am = [Buf("Am%d" % i) for i in range(2)]
            scT = [sbd("scT%d" % i, [64, 4, 64], BF16) for i in range(2)]; B_scT = [Buf("scT%d" % i) for i in range(2)]
            qs = [sbd("qs%d" % i, [128, 8, 64], BF16) for i in range(2)]; B_qs = [Buf("qs%d" % i) for i in range(2)]
            kw = [sbd("kw%d" % i, [64, 4, 256], BF16) for i in range(2)]; B_kw = [Buf("kw%d" % i) for i in range(2)]
            rd = [sbd("rd%d" % i, [64, 16], F32) for i in range(2)]; B_rd = [Buf("rd%d" % i) for i in range(2)]
            junkd = sbd("junkd", [64, 256], BF16); B_junkd = Buf("junkd")
            ym = [sbd("ym%d" % i, [64, 1024], BF16) for i in range(2)]; B_ym = [Buf("ym%d" % i) for i in range(2)]
            P.dma("sp", gml[:], bcast(mlstm_norm_g[0:1, :], 64), w=[B_gml])
            P.op("dve", lambda e: e.memset(Cst[:], 0.0), w=B_C)
            P.op("dve", lambda e: e.memset(Cb[:], 0.0), w=B_Cb)
            P.op("dve", lambda e: e.memset(nvec[:], 0.0), w=[B_nv])
            P.op("dve", lambda e: e.memset(nbv[:], 0.0), w=[B_nbv])
            for i in range(2):
                P.op("dve", lambda e, i=i: e.memset(vt[i][:, :, 256:257], 1.0), w=[B_vt[i]])
            PSk = PS[5][:].bitcast(BF16)

            def pre(n):
                i = n % 2
                c0_ = n * 64
                pi = (n // PCH) % 2
                lo = (n % PCH) * 64
                P.dma("sp", qk[i][:], QKC[:, :, c0_:c0_ + 64].rearrange("b d t -> d b t"), r=[B_QKC], w=[B_qk[i]])
                P.dma("sp", vt[i][:, :, 0:256], VM[c0_:c0_ + 64, :].rearrange("t (h e) -> t h e", h=4), r=[B_VM],
                      w=[B_vt[i]])
                def gtr(e):
                    ins = None
                    for q_, src in enumerate((Ug[pi], WSg[pi], FLg[pi])):
                        ins = e.transpose(out=PS[0][0:64, 256 + 4 * q_:260 + 4 * q_], in_=src[0:4, lo:lo + 64],
                                          identity=ident_f[0:4, 0:4])
                    return ins
                P.op("pe", gtr, r=[B_gp[pi], B_const], w=[B_PS[0]])
                P.op("act", lambda e: e.copy(out=GT[i][:], in_=PS[0][0:64, 256:268]), r=[B_PS[0]], w=[B_GT[i]])
                def ktr(e):
                    ins = None
                    for b8 in range(8):
                        ins = e.transpose(out=PSk[0:64, b8 * 128:(b8 + 1) * 128], in_=qk[i][:, 8 + b8, :],
                                          identity=ident_b[:])
                    return ins
                P.op("pe", ktr, r=[B_qk[i], B_const], w=[B_PS[5]])
                for hd in range(4):
                    P.op("dve", lambda e, hd=hd: e.tensor_scalar(out=kw[i][:, hd, :],
                                                                 in0=PSk[0:64, hd * 256:(hd + 1) * 256],
                                                                 scalar1=GT[i][:, 4 + hd:5 + hd], scalar2=None,
                                                                 op0=ALU.mult),
                         r=[B_PS[5], B_GT[i]], w=[B_kw[i]])
                def bmm(e):
                    ins = None
                    for hd in range(4):
                        ins = e.matmul(PS[1][0:64, hd * 64:(hd + 1) * 64], lhsT=OH[:, hd, 0:64],
                                       rhs=CMg[pi][0:4, lo:lo + 64], start=True, stop=True)
                    for hd in range(4):
                        ins = e.matmul(PS[1][:, 256 + hd * 64:256 + (hd + 1) * 64], lhsT=OH[:, hd, :],
                                       rhs=WIg[pi][0:4, lo:lo + 64], start=True, stop=True)
                    return ins
                P.op("pe", bmm, r=[B_OH, B_gp[pi]], w=[B_PS[1]])
                def smm(e):
                    ins = None
                    for hd in range(4):
                        for dc in range(2):
                            ins = e.matmul(PS[0][0:64, hd * 64:(hd + 1) * 64], lhsT=qk[i][:, 8 + 2 * hd + dc, :],
                                           rhs=qk[i][:, 2 * hd + dc, :], start=(hd == 0 and dc == 0),
                                           stop=(hd == 3 and dc == 1), skip_group_check=True)
                    return ins
                P.op("pe", smm, r=[B_qk[i]], w=[B_PS[0]])
                for hd in range(4):
                    P.op("act", lambda e, hd=hd: e.activation(out=Am[i][:, hd, :],
                                                              in_=PS[1][0:64, hd * 64:(hd + 1) * 64],
                                                              func=AF.Exp, bias=GT[i][:, hd:hd + 1], scale=1.0),
                         r=[B_PS[1], B_GT[i]], w=[B_Am[i]])
                P.op("pool", lambda e: e.affine_select(out=Am[i][:], in_=Am[i][:], pattern=[[0, 4], [1, 64]],
                                                       compare_op=ALU.is_ge, fill=freg(e, 0.0), base=0,
                                                       channel_multiplier=-1), r=[B_Am[i]], w=[B_Am[i]])
                P.op("dve", lambda e: e.tensor_tensor(out=scT[i][:].rearrange("p h j -> p (h j)"),
                                                      in0=Am[i][:].rearrange("p h j -> p (h j)"),
                                                      in1=PS[0][0:64, 0:256], op=ALU.mult),
                     r=[B_Am[i], B_PS[0]], w=[B_scT[i]])
                for dc in range(2):
                    P.op("dve", lambda e, dc=dc: e.tensor_tensor(
                        out=qs[i][:, dc:8:2, :], in0=qk[i][:, dc:8:2, :],
                        in1=PS[1][:, 256:512].rearrange("p (h j) -> p h j", h=4), op=ALU.mult),
                        r=[B_qk[i], B_PS[1]], w=[B_qs[i]])

            def main(n):
                i = n % 2
                c0_ = n * 64
                pi = (n // PCH) % 2
                nl = n % PCH
                for hd in range(4):
                    def hmm(e, hd=hd):
                        o_ = PS[2 + hd // 2][0:64, (hd % 2) * 256:(hd % 2) * 256 + 256]
                        ins = None
                        for dc in range(2):
                            ins = e.matmul(o_, lhsT=qs[i][:, 2 * hd + dc, :], rhs=Cb[:, 2 * hd + dc, :],
                                           start=(hd % 2 == 0 and dc == 0), stop=False, skip_group_check=True)
                        ins = e.matmul(o_, lhsT=scT[i][:, hd, :], rhs=vt[i][:, hd, 0:256], start=False,
                                       stop=(hd % 2 == 1), skip_group_check=True)
                        return ins
                    P.op("pe", hmm, r=[B_qs[i], B_Cb[hd], B_scT[i], B_vt[i]], w=[B_PS[2 + hd // 2]])

                def dmm(e):
                    ins = None
                    for hd in range(4):
                        o_ = PS[4][0:64, hd:hd + 1]
                        for dc in range(2):
                            ins = e.matmul(o_, lhsT=qs[i][:, 2 * hd + dc, :], rhs=nbv[:, 2 * hd + dc:2 * hd + dc + 1],
                                           start=(hd == 0 and dc == 0), stop=False, skip_group_check=True)
                        ins = e.matmul(o_, lhsT=scT[i][:, hd, :], rhs=vt[i][:, hd, 256:257], start=False,
                                       stop=(hd == 3), skip_group_check=True)
                    return ins
                P.op("pe", dmm, r=[B_qs[i], B_scT[i], B_vt[i], B_nbv], w=[B_PS[4]])
                R_ = rd[i]
                P.op("dve", lambda e: e.tensor_copy(out=R_[:, 4:8], in_=PS[4][0:64, 0:4]), r=[B_PS[4]], w=[B_rd[i]])
                for hd in range(4):
                    ub = 6 + hd % 2

                    def umm(e, hd=hd, ub=ub):
                        ins = None
                        for dc in range(2):
                            ins = e.matmul(PS[ub][:, dc * 256:(dc + 1) * 256], lhsT=kw[i][:, hd, dc * 128:(dc + 1) * 128],
                                           rhs=vt[i][:, hd, 0:256], start=True, stop=True)
                        return ins
                    P.op("pe", umm, r=[B_kw[i], B_vt[i]], w=[B_PS[ub]])
                    P.op("dve", lambda e, hd=hd, ub=ub: e.scalar_tensor_tensor(
                        out=Cst[:, 2 * hd:2 * hd + 2, :], in0=Cst[:, 2 * hd:2 * hd + 2, :],
                        scalar=DECB[pi][:, hd, nl:nl + 1], in1=PS[ub][:, :].rearrange("p (a b) -> p a b", a=2),
                        op0=ALU.mult, op1=ALU.add), r=[B_C[hd], B_DECB[pi], B_PS[ub]], w=[B_C[hd]])
                    P.op("act", lambda e, hd=hd: e.copy(out=Cb[:, 2 * hd:2 * hd + 2, :], in_=Cst[:, 2 * hd:2 * hd + 2, :]),
                         r=[B_C[hd]], w=[B_Cb[hd]])

                def nmm(e):
                    ins = None
                    for hd in range(4):
                        for dc in range(2):
                            ins = e.matmul(PS[4][:, 128 + 2 * hd + dc:129 + 2 * hd + dc],
                                           lhsT=kw[i][:, hd, dc * 128:(dc + 1) * 128], rhs=vt[i][:, hd, 256:257],
                                           start=True, stop=True)
                    return ins
                P.op("pe", nmm, r=[B_kw[i], B_vt[i]], w=[B_PS[4]])
                for hd in range(4):
                    P.op("dve", lambda e, hd=hd: e.scalar_tensor_tensor(
                        out=nvec[:, 2 * hd:2 * hd + 2], in0=nvec[:, 2 * hd:2 * hd + 2],
                        scalar=DECB[pi][:, hd, nl:nl + 1], in1=PS[4][:, 128 + 2 * hd:130 + 2 * hd],
                        op0=ALU.mult, op1=ALU.add), r=[B_nv, B_DECB[pi], B_PS[4]], w=[B_nv])
                P.op("pool", lambda e: e.tensor_copy(out=nbv[:], in_=nvec[:]), r=[B_nv], w=[B_nbv])
                P.op("dve", lambda e: e.scalar_tensor_tensor(out=R_[:, 0:4], in0=R_[:, 4:8], scalar=-1.0, in1=R_[:, 4:8],
                                                             op0=ALU.mult, op1=ALU.max), r=[B_rd[i]], w=[B_rd[i]])
                P.op("dve", lambda e: e.tensor_tensor(out=R_[:, 0:4], in0=R_[:, 0:4], in1=GT[i][:, 8:12], op=ALU.max),
                     r=[B_rd[i], B_GT[i]], w=[B_rd[i]])
                P.op("dve", lambda e: e.reciprocal(out=R_[:, 0:4], in_=R_[:, 0:4]), r=[B_rd[i]], w=[B_rd[i]])
                for hd in range(4):
                    P.op("act", lambda e, hd=hd: e.activation(
                        out=junkd[:], in_=PS[2 + hd // 2][0:64, (hd % 2) * 256:(hd % 2) * 256 + 256], func=AF.Square,
                        accum_out=R_[:, 4 + hd:5 + hd]), r=[B_PS[2 + hd // 2], B_rd[i]], w=[B_junkd, B_rd[i]])
                P.op("dve", lambda e: e.tensor_tensor(out=R_[:, 8:12], in0=R_[:, 0:4], in1=R_[:, 0:4], op=ALU.mult),
                     r=[B_rd[i]], w=[B_rd[i]])
                P.op("dve", lambda e: e.tensor_tensor(out=R_[:, 8:12], in0=R_[:, 8:12], in1=R_[:, 4:8], op=ALU.mult),
                     r=[B_rd[i]], w=[B_rd[i]])
                P.op("act", lambda e: e.activation(out=R_[:, 8:12], in_=R_[:, 8:12], func=AF.Sqrt, bias=eps_t[0:64, 0:1],
                                                   scale=1.0 / 256), r=[B_rd[i], B_const], w=[B_rd[i]])
                P.op("dve", lambda e: e.reciprocal(out=R_[:, 8:12], in_=R_[:, 8:12]), r=[B_rd[i]], w=[B_rd[i]])
                P.op("dve", lambda e: e.tensor_tensor(out=R_[:, 12:16], in0=R_[:, 8:12], in1=R_[:, 0:4], op=ALU.mult),
                     r=[B_rd[i]], w=[B_rd[i]])
                for hd in range(4):
                    P.op("dve", lambda e, hd=hd: e.scalar_tensor_tensor(
                        out=ym[i][:, hd * 256:(hd + 1) * 256],
                        in0=PS[2 + hd // 2][0:64, (hd % 2) * 256:(hd % 2) * 256 + 256], scalar=R_[:, 12 + hd:13 + hd],
                        in1=gml[:, hd * 256:(hd + 1) * 256], op0=ALU.mult, op1=ALU.mult),
                        r=[B_PS[2 + hd // 2], B_rd[i], B_gml], w=[B_ym[i]])
                row0 = ((n // 2) % 2) * SO + ((n // 2) // 2) * 128 + (n % 2) * 64
                P.dma("pool", YMP[row0:row0 + 64, :], ym[i][:], r=[B_ym[i]], w=[B_YMP])

            gates(0)
            pre(0)
            for n in range(NCH):
                if n % PCH == PCH // 2 and n // PCH + 1 < NPC:
                    gates(n // PCH + 1)
                if n + 1 < NCH:
                    pre(n + 1)
                main(n)
        P.barrier()
        if "C" in stages:
          with contextlib.ExitStack() as st_c:
            def sbc(name, shape, dt):
                return st_c.enter_context(nc.sbuf_tensor(name, list(shape), dt))

            ND = NT + 16 * (NCT - 1)
            EXPM = sbc("EXPM", [128, S], BF16)
            NLS = NT + 1
            Lsel = sbc("Lsel", [128, NLS, 128], BF16)
            Lcmp = sbc("Lcmp", [128, ND, 128], BF16)
            Rsel = [sbc("Rsel%d" % g, [128, 4, 128], BF16) for g in range(2)]
            Rcmp = [sbc("Rcmp%d" % g, [128, 4, 128], BF16) for g in range(2)]
            OV = sbc("OV", [128, NCT, 128], BF16)
            qrow = sbc("qrow", [128, 128], F32)
            B_cc = Buf("cconst")
            pl = lambda fn, **k: P.op("pool", fn, r=[B_cc], w=[B_cc])
            P.op("dve", lambda e: e.memset(EXPM[:], 1.0), r=[B_cc], w=[B_cc])
            pl(lambda e: e.affine_select(out=EXPM[:], in_=EXPM[:], pattern=[[1, S]], compare_op=ALU.is_ge, fill=freg(e, 0.0),
                                         base=0, channel_multiplier=-64))
            pl(lambda e: e.affine_select(out=EXPM[:], in_=EXPM[:], pattern=[[-1, S]], compare_op=ALU.is_ge, fill=freg(e, 0.0),
                                         base=63, channel_multiplier=64))
            P.op("dve", lambda e: e.memset(Lsel[:], 0.0), r=[B_cc], w=[B_cc])
            P.op("dve", lambda e: e.memset(Lcmp[:], 0.0), r=[B_cc], w=[B_cc])
            pl(lambda e: e.iota(Lsel[0:1, :, :], pattern=[[0, NLS], [1, 128]], base=0, channel_multiplier=0,
                                allow_small_or_imprecise_dtypes=True))
            pl(lambda e: e.memset(Lsel[32:33, :, :], 1.0))
            pl(lambda e: e.iota(Lsel[64:65, :, :], pattern=[[128, NLS], [0, 128]], base=-128, channel_multiplier=0,
                                allow_small_or_imprecise_dtypes=True))
            pl(lambda e: e.iota(Lcmp[0:1, :, :], pattern=[[0, ND], [16, 128]], base=0, channel_multiplier=0,
                                allow_small_or_imprecise_dtypes=True))
            pl(lambda e: e.memset(Lcmp[32:33, :, :], 1.0))
            pl(lambda e: e.iota(Lcmp[64:65, :, :], pattern=[[128, ND], [0, 128]], base=-128 * 16 * (NCT - 1),
                                channel_multiplier=0, allow_small_or_imprecise_dtypes=True))
            pl(lambda e: e.iota(qrow[:], pattern=[[1, 128]], base=0, channel_multiplier=0,
                                allow_small_or_imprecise_dtypes=True))
            hb = sbc("hb", [128, 6], F32)
            B_hb = Buf("hb")
            P.dma("sp", hb[:, 0:1], bcast(hsel[0:1, 0:1]), w=[B_hb])
            P.op("dve", lambda e: e.tensor_scalar(out=hb[:, 1:2], in0=hb[:, 0:1], scalar1=128.0, scalar2=None,
                                                  op0=ALU.mult), r=[B_hb], w=[B_hb])
            P.op("dve", lambda e: e.tensor_scalar(out=hb[:, 2:3], in0=hb[:, 0:1], scalar1=2.0, scalar2=None,
                                                  op0=ALU.mult), r=[B_hb], w=[B_hb])
            P.op("dve", lambda e: e.tensor_scalar(out=hb[:, 3:4], in0=hb[:, 0:1], scalar1=-1.0, scalar2=1.0,
                                                  op0=ALU.mult, op1=ALU.add), r=[B_hb], w=[B_hb])
            P.op("dve", lambda e: e.tensor_scalar(out=hb[:, 4:5], in0=hb[:, 0:1], scalar1=-BIGM, scalar2=None,
                                                  op0=ALU.mult), r=[B_hb], w=[B_hb])
            P.op("dve", lambda e: e.tensor_scalar(out=hb[:, 5:6], in0=hb[:, 3:4], scalar1=-BIGM, scalar2=None,
                                                  op0=ALU.mult), r=[B_hb], w=[B_hb])
            P.op("dve", lambda e: e.tensor_scalar(out=qrow[:], in0=qrow[:], scalar1=hb[:, 1:2], scalar2=None,
                                                  op0=ALU.add), r=[B_hb, B_cc], w=[B_cc])
            Mdiag = sbc("Mdiag", [128, 128], F32); Manti = sbc("Manti", [128, 128], F32)
            WM = {r_: sbc("WM%d" % r_, [128, 4, 128], BF16) for r_ in (0, 1, 4, 5)}
            P.op("dve", lambda e: e.memset(Mdiag[:], 0.0), r=[B_cc], w=[B_cc])
            P.op("dve", lambda e: e.memset(Manti[:], 0.0), r=[B_cc], w=[B_cc])
            pl(lambda e: e.affine_select(out=Mdiag[:], in_=Mdiag[:], pattern=[[1, 128]], compare_op=ALU.is_ge,
                                         fill=freg(e, -BIGM), base=0, channel_multiplier=-1))
            pl(lambda e: e.affine_select(out=Manti[:], in_=Manti[:], pattern=[[-1, 128]], compare_op=ALU.is_ge,
                                         fill=freg(e, -BIGM), base=-1, channel_multiplier=1))
            dvc = lambda fn: P.op("dve", fn, r=[B_cc, B_hb], w=[B_cc])
            for hh in range(4):
                dvc(lambda e, hh=hh: e.tensor_scalar(out=WM[0][:, hh, :], in0=Manti[:], scalar1=hb[:, 3:4],
                                                     scalar2=hb[:, 4:5], op0=ALU.mult, op1=ALU.add))
                dvc(lambda e, hh=hh: e.tensor_scalar(out=WM[1][:, hh, :], in0=Manti[:], scalar1=hb[:, 0:1],
                                                     scalar2=None, op0=ALU.mult))
                dvc(lambda e, hh=hh: e.tensor_scalar(out=WM[4][:, hh, :], in0=Mdiag[:], scalar1=hb[:, 3:4],
                                                     scalar2=None, op0=ALU.mult))
                dvc(lambda e, hh=hh: e.tensor_scalar(out=WM[5][:, hh, :], in0=Mdiag[:], scalar1=hb[:, 0:1],
                                                     scalar2=hb[:, 5:6], op0=ALU.mult, op1=ALU.add))
            for g in range(2):
                pl(lambda e, g=g: e.memset(Rsel[g][:], 0.0))
                pl(lambda e, g=g: e.memset(Rcmp[g][:], 0.0))
                for r in range(4):
                    sl = 2.0 ** (-(4 * g + r + 1))
                    for R_ in (Rsel[g], Rcmp[g]):
                        pl(lambda e, R_=R_, r=r, sl=sl: e.memset(R_[0:1, r, :], sl))
                        pl(lambda e, R_=R_, r=r, sl=sl: e.memset(R_[64:65, r, :], -sl))
                    pl(lambda e, g=g, r=r, sl=sl: e.tensor_scalar(out=Rsel[g][32:33, r, :], in0=qrow[32:33, :],
                                                                  scalar1=-sl, scalar2=None, op0=ALU.mult))
                    pl(lambda e, g=g, r=r, sl=sl: e.tensor_scalar(out=Rcmp[g][32:33, r, :], in0=qrow[32:33, :],
                                                                  scalar1=-31.0, scalar2=-sl, op0=ALU.add, op1=ALU.mult))
            P.op("dve", lambda e: e.memset(OV[:], 1.0), r=[B_cc], w=[B_cc])
            pl(lambda e: e.affine_select(out=OV[:], in_=OV[:], pattern=[[2048, NCT], [-64, 128]], compare_op=ALU.is_ge,
                                         fill=freg(e, 0.0), base=31, channel_multiplier=16))
            pl(lambda e: e.affine_select(out=OV[:], in_=OV[:], pattern=[[-2048, NCT], [64, 128]], compare_op=ALU.is_ge,
                                         fill=freg(e, 0.0), base=48, channel_multiplier=-16))
            NJ = NT // 2
            J0 = sbc("J0", [128, 128], F32); CJ = sbc("CJ", [128, NJ], F32); E0 = sbc("E0", [128, 128], F32)
            V0 = sbc("V0", [128, 128], F32); OFFT = sbc("OFFT", [128, NJ * NCT], F32)
            cb = sbc("cb", [128, 128], F32); B_cb = Buf("cb")
            cm01 = [sbc("cm01_%d" % i, [128, 128], F32) for i in range(2)]; B_cm01 = [Buf("cm01_%d" % i) for i in range(2)]
            cmw = [sbc("cmw%d" % i, [128, 4, 128], BF16) for i in range(2)]; B_cmw = [Buf("cmw%d" % i) for i in range(2)]
            cmw_i = [0]
            ia = dict(channel_multiplier=0, allow_small_or_imprecise_dtypes=True)
            pl(lambda e: e.iota(J0[:], pattern=[[1, 128]], base=0, **ia))
            pl(lambda e: e.tensor_scalar(out=J0[64:128, :], in0=J0[64:128, :], scalar1=-1.0, scalar2=1.0,
                                         op0=ALU.add, op1=ALU.mult))
            pl(lambda e: e.iota(CJ[:], pattern=[[4, NJ]], base=0, **ia))
            P.op("dve", lambda e: e.tensor_scalar(out=CJ[:], in0=CJ[:], scalar1=hb[:, 2:3], scalar2=None, op0=ALU.add),
                 r=[B_cc, B_hb], w=[B_cc])
            P.op("dve", lambda e: e.memset(E0[:], 0.0), r=[B_cc], w=[B_cc])
            P.op("dve", lambda e: e.memset(E0[:, 0:1], 1.0), r=[B_cc], w=[B_cc])
            pl(lambda e: e.iota(V0[:], pattern=[[1, 128]], base=0, channel_multiplier=-16,
                                allow_small_or_imprecise_dtypes=True))
            pl(lambda e: e.iota(OFFT[:], pattern=[[256, NJ], [-2048, NCT]], base=-31, **ia))
            P.op("dve", lambda e: e.tensor_scalar(out=OFFT[:], in0=OFFT[:], scalar1=hb[:, 1:2], scalar2=None,
                                                  op0=ALU.add), r=[B_cc, B_hb], w=[B_cc])

            KsT = sbc("KsT", [128, S], BF16); B_KsT = Buf("KsT")
            KwT = sbc("KwT", [128, S], BF16); B_KwT = Buf("KwT")
            Vs = sbc("Vs", [128, NT, 129], BF16); B_Vs = Buf("Vs")
            Vw = sbc("Vw", [128, NT, 129], BF16); B_Vw = Buf("Vw")
            KCt = sbc("KCt", [128, NCT * 128], BF16); B_KCt = Buf("KCt")
            VCt = sbc("VCt", [128, NCT, 129], BF16); B_VCt = Buf("VCt")
            kin = KsT; B_kin = B_KsT
            W1c = sbc("W1c", [128, 32, 128], BF16); B_W1c = Buf("W1c")
            W2c = sbc("W2c", [128, 128], BF16); B_W2c = Buf("W2c")
            posf = sbc("posf", [32, 128], F32); B_posf = Buf("posf")
            posT = sbc("posT", [128, 32], BF16); B_posT = Buf("posT")
            c1 = sbc("c1", [128, 1], F32); B_c1 = Buf("c1")
            hidt = sbc("hidt", [128, NCT * 128], BF16); B_hidt = Buf("hidt")
            qt = [sbc("qt%d" % i, [128, 4, 128], BF16) for i in range(2)]; B_qt = [Buf("qt%d" % i) for i in range(2)]
            gt = [sbc("gt%d" % i, [128, 24], F32) for i in range(2)]; B_gt = [Buf("gt%d" % i) for i in range(2)]
            yt = [sbc("yt%d" % i, [128, 4, 128], F32) for i in range(2)]; B_yt = [Buf("yt%d" % i) for i in range(2)]
            ybf = [sbc("ybf%d" % i, [128, 512], BF16) for i in range(2)]; B_ybf = [Buf("ybf%d" % i) for i in range(2)]
            PT = [sbc("PT%d" % i, [128, 4, 128], BF16) for i in range(4)]; B_PT = [Buf("PT%d" % i) for i in range(4)]
            dn = sbc("dn", [128, 12], F32); B_dn = Buf("dn")
            imp = sbc("imp", [128, 128], F32); B_imp = Buf("imp")
            imp2 = sbc("imp2", [128, 128], F32); B_imp2 = Buf("imp2")
            m8 = sbc("m8", [128, 16], F32); B_m8 = Buf("m8")
            selq = sbc("selq", [128, 128], BF16); B_selq = Buf("selq")
            selb = sbc("selb", [128, 4, 128], BF16); B_selb = Buf("selb")
            pti = [0]
            sbi = [0]
            obi = [0]

            P.op("dve", lambda e: e.memset(hidt[:], 0.0), w=[B_hidt])

            def attn_branch(tiles, br, g, ci, is_cmp):
                n = len(tiles)
                ob = [(2, 3), (6, 7)][obi[0] % 2]
                obi[0] += 1
                slots = []

                def emit_S(i):
                    t = tiles[i]
                    sbk = (0, 1, 5)[sbi[0] % 3]
                    sbi[0] += 1
                    cmi = t.get("cmask")
                    if cmi is not None:
                        mi = cmw_i[0] % 2
                        cmw_i[0] += 1
                        P.op("dve", lambda e, mi=mi, cmi=cmi: e.tensor_scalar(
                            out=cm01[mi][:], in0=V0[:], scalar1=OFFT[:, cmi:cmi + 1], scalar2=0.0, op0=ALU.add,
                            op1=ALU.is_ge), r=[B_cc], w=[B_cm01[mi]])
                        for hh in range(4):
                            P.op("dve", lambda e, hh=hh, mi=mi: e.tensor_scalar(
                                out=cmw[mi][:, hh, :], in0=cm01[mi][:], scalar1=BIGM, scalar2=-BIGM, op0=ALU.mult,
                                op1=ALU.add), r=[B_cm01[mi]], w=[B_cmw[mi]])
                        t = dict(t)
                        t["wm"] = cmw[mi]
                        t["Bwm"] = B_cmw[mi]

                    def f(e, t=t, sbk=sbk):
                        e.matmul(PS[sbk][:, :], lhsT=t["kT"], rhs=qt[ci][:].rearrange("p h q -> p (h q)"),
                                 start=True, stop=False)
                        has_sel = t.get("sel") is not None
                        has_wm = t.get("wm") is not None
                        ins = e.matmul(PS[sbk][:, :], lhsT=t["L"], rhs=t["R"].rearrange("p h q -> p (h q)"),
                                       start=False, stop=not (has_sel or has_wm))
                        if has_sel:
                            ins = e.matmul(PS[sbk][:, :], lhsT=t["sel"], rhs=selb[:].rearrange("p h q -> p (h q)"),
                                           start=False, stop=not has_wm)
                        if has_wm:
                            ins = e.matmul(PS[sbk][:, :], lhsT=ident_b[:], rhs=t["wm"][:].rearrange("p h q -> p (h q)"),
                                           start=False, stop=True)
                        return ins

                    rb = [t["Bk"], B_qt[ci], B_cc, B_const] + ([B_selb] if t.get("sel") is not None else []) + (
                        [t["Bwm"]] if t.get("Bwm") is not None else [])
                    P.op("pe", f, r=rb, w=[B_PS[sbk]])
                    return sbk

                def emit_exp(i, sbk):
                    k = pti[0] % 4
                    pti[0] += 1
                    P.op("act", lambda e, k=k, sbk=sbk: e.activation(out=PT[k][:].rearrange("p h q -> p (h q)"),
                                                                     in_=PS[sbk][:, :], func=AF.Exp),
                         r=[B_PS[sbk]], w=[B_PT[k]])
                    return k

                def emit_pv(i, k):
                    t = tiles[i]

                    def f(e, t=t, k=k, i=i):
                        ins = None
                        for h in range(4):
                            ins = e.matmul(PS[ob[h // 2]][:, (h % 2) * 129:(h % 2) * 129 + 129], lhsT=PT[k][:, h, :],
                                           rhs=t["V"], start=(i == 0 and h % 2 == 0), stop=(i == n - 1),
                                           skip_group_check=True)
                        if is_cmp:
                            for h in range(4):
                                ins = e.matmul(PS[4][:, h * 128:(h + 1) * 128], lhsT=PT[k][:, h, :], rhs=t["ov"],
                                               start=(i == 0 and h == 0), stop=(i == n - 1), skip_group_check=True)
                        return ins

                    wb = [B_PS[ob[0]], B_PS[ob[1]]] + ([B_PS[4]] if is_cmp else [])
                    P.op("pe", f, r=[B_PT[k], t["Bv"], B_cc], w=wb)

                sbq = [emit_S(i) for i in range(min(2, n))]
                for i in range(n):
                    if i + 2 < n:
                        sbq.append(emit_S(i + 2))
                    k = emit_exp(i, sbq[i])
                    emit_pv(i, k)
                o0 = 4 * br
                for bkk in range(2):
                    P.op("dve", lambda e, bkk=bkk: e.tensor_scalar(
                        out=dn[:, o0 + 2 * bkk:o0 + 2 * bkk + 2],
                        in0=PS[ob[bkk]][:, 0:258].rearrange("p (h c) -> p h c", h=2)[:, :, 128],
                        scalar1=1e-30, scalar2=None, op0=ALU.max), r=[B_PS[ob[bkk]]], w=[B_dn])
                P.op("dve", lambda e: e.reciprocal(out=dn[:, o0:o0 + 4], in_=dn[:, o0:o0 + 4]), r=[B_dn], w=[B_dn])
                if is_cmp:
                    for h in range(4):
                        if h == 0:
                            P.op("dve", lambda e: e.tensor_scalar(out=imp[:], in0=PS[4][:, 0:128], scalar1=dn[:, o0:o0 + 1],
                                                                  scalar2=None, op0=ALU.mult),
                                 r=[B_PS[4], B_dn], w=[B_imp])
                        else:
                            P.op("dve", lambda e, h=h: e.scalar_tensor_tensor(
                                out=imp[:], in0=PS[4][:, h * 128:(h + 1) * 128], scalar=dn[:, o0 + h:o0 + h + 1],
                                in1=imp[:], op0=ALU.mult, op1=ALU.add), r=[B_PS[4], B_dn, B_imp], w=[B_imp])
                gcol = 12 * g + br
                P.op("dve", lambda e: e.tensor_tensor(out=dn[:, o0:o0 + 4], in0=dn[:, o0:o0 + 4],
                                                      in1=gt[ci][:, gcol:gcol + 10:3], op=ALU.mult),
                     r=[B_dn, B_gt[ci]], w=[B_dn])
                for h in range(4):
                    src = PS[ob[h // 2]][:, (h % 2) * 129:(h % 2) * 129 + 128]
                    if br == 0:
                        P.op("dve", lambda e, h=h, src=src: e.tensor_scalar(
                            out=yt[ci][:, h, :], in0=src, scalar1=dn[:, o0 + h:o0 + h + 1], scalar2=None, op0=ALU.mult),
                            r=[B_PS[ob[h // 2]], B_dn], w=[B_yt[ci]])
                    else:
                        P.op("dve", lambda e, h=h, src=src: e.scalar_tensor_tensor(
                            out=yt[ci][:, h, :], in0=src, scalar=dn[:, o0 + h:o0 + h + 1], in1=yt[ci][:, h, :],
                            op0=ALU.mult, op1=ALU.add), r=[B_PS[ob[h // 2]], B_dn, B_yt[ci]], w=[B_yt[ci]])

            for g in range(2):
                for kv in range(2):
                    w1src = (w_cmp_k1, w_cmp_v1)[kv]
                    w2src = (w_cmp_k2, w_cmp_v2)[kv]
                    possrc = (pos_cmp_k, pos_cmp_v)[kv]
                    P.dma("pool", W1c[:], w1src.rearrange("(r d) h -> d r h", d=128), w=[B_W1c])
                    P.dma("pool", W2c[:], w2src, w=[B_W2c])
                    P.dma("sp", posf[:], possrc, w=[B_posf])
                    P.dma("sp", kin[:], KCV[2 * kv + g, :, :], r=[B_KCV], w=[B_kin])
                    P.op("pe", lambda e: e.transpose(out=PS[5][:, 0:32], in_=posf[:], identity=ident_f[0:32, 0:32]),
                         r=[B_posf, B_const], w=[B_PS[5]])
                    P.op("dve", lambda e: e.tensor_copy(out=posT[:], in_=PS[5][:, 0:32]), r=[B_PS[5]], w=[B_posT])

                    def c1mm(e):
                        ins = None
                        for r in range(32):
                            ins = e.matmul(PS[5][:, 64:65], lhsT=W1c[:, r, :], rhs=posT[:, r:r + 1], start=(r == 0),
                                           stop=(r == 31))
                        return ins
                    P.op("pe", c1mm, r=[B_W1c, B_posT], w=[B_PS[5]])
                    P.op("dve", lambda e: e.tensor_copy(out=c1[:], in_=PS[5][:, 64:65]), r=[B_PS[5]], w=[B_c1])

                    def hmm(e):
                        ins = None
                        for r in range(32):
                            ins = e.matmul(PS[0][:, 0:NCMP], lhsT=W1c[:, r, :],
                                           rhs=kin[:, r:r + 16 * (NCMP - 1) + 1:16], start=(r == 0), stop=(r == 31))
                        return ins
                    P.op("pe", hmm, r=[B_W1c, B_kin], w=[B_PS[0]])
                    P.op("act", lambda e: e.activation(out=hidt[:, 0:NCMP], in_=PS[0][:, 0:NCMP], func=AF.Silu,
                                                       bias=c1[:, 0:1]), r=[B_PS[0], B_c1], w=[B_hidt])
                    if kv == 0:
                        P.op("pe", lambda e: e.matmul(PS[1][:, 0:NCT * 128], lhsT=W2c[:], rhs=hidt[:], start=True,
                                                      stop=True), r=[B_W2c, B_hidt], w=[B_PS[1]])
                        P.op("dve", lambda e: e.tensor_copy(out=KCt[:], in_=PS[1][:, 0:NCT * 128]), r=[B_PS[1]],
                             w=[B_KCt])
                    else:
                        def vmm(e):
                            ins = None
                            for tn in range(NCT):
                                ins = e.matmul(PS[1][:, tn * 128:(tn + 1) * 128], lhsT=hidt[:, tn * 128:(tn + 1) * 128],
                                               rhs=W2c[:], start=True, stop=True)
                            return ins
                        P.op("pe", vmm, r=[B_W2c, B_hidt], w=[B_PS[1]])
                        P.op("dve", lambda e: e.tensor_copy(
                            out=VCt[:, :, 0:128], in_=PS[1][:, 0:NCT * 128].rearrange("p (t c) -> p t c", t=NCT)),
                            r=[B_PS[1]], w=[B_VCt])
                        P.op("dve", lambda e: e.memset(VCt[:, :, 128:129], 1.0), r=[B_VCt], w=[B_VCt])
                P.dma("sp", KsT[:], KST[g, :, :], r=[B_KST], w=[B_KsT])
                P.dma("sp", KwT[:], KWT[g, :, :], r=[B_KWT], w=[B_KwT])
                P.dma("sp", Vs[:, :, 0:128], VS[:, g * 128:(g + 1) * 128].rearrange("(t p) c -> p t c", p=128),
                      r=[B_VS], w=[B_Vs])
                P.op("dve", lambda e: e.memset(Vs[:, :, 128:129], 1.0), r=[B_Vs], w=[B_Vs])
                P.dma("pool", Vw[:, :, 0:128], VWG[:, g * 128:(g + 1) * 128].rearrange("(t p) c -> p t c", p=128),
                      r=[B_VWG], w=[B_Vw])
                P.op("dve", lambda e: e.memset(Vw[:, :, 128:129], 1.0), r=[B_Vw], w=[B_Vw])
                for j in range(NT // 2):
                    ci = j % 2
                    P.dma("sp", qt[ci][:], QT[4 * g:4 * g + 4, :, j * 128:(j + 1) * 128].rearrange("h d t -> d h t"),
                          r=[B_QT], w=[B_qt[ci]])
                    P.dma("sp", gt[ci][:], GOWN[j * 128:(j + 1) * 128, :], r=[B_GOWN], w=[B_gt[ci]])
                    P.op("act", lambda e, ci=ci: e.activation(out=gt[ci][:], in_=gt[ci][:], func=AF.Exp, scale=-1.0),
                         r=[B_gt[ci]], w=[B_gt[ci]])
                    P.op("dve", lambda e, ci=ci: e.tensor_scalar(out=gt[ci][:], in0=gt[ci][:], scalar1=1.0, scalar2=None,
                                                                 op0=ALU.add), r=[B_gt[ci]], w=[B_gt[ci]])
                    P.op("dve", lambda e, ci=ci: e.reciprocal(out=gt[ci][:], in_=gt[ci][:]), r=[B_gt[ci]], w=[B_gt[ci]])
                    tmax = min(NCT - 1, (8 * (2 * j + 1) + 6) // 128)
                    tiles = []
                    for tn in range(tmax + 1):
                        full = (128 * tn + 127) <= (16 * j - 2)
                        dl = 2 * j - 16 * tn + 16 * (NCT - 1)
                        tiles.append(dict(kT=KCt[:, tn * 128:(tn + 1) * 128], Bk=B_KCt, L=Lcmp[:, dl, :], R=Rcmp[g][:],
                                          V=VCt[:, tn, :], Bv=B_VCt, ov=OV[:, tn, :],
                                          cmask=None if full else (j * NCT + tn)))
                    attn_branch(tiles, 0, g, ci, True)
                    dq = lambda fn, rr=(), ww=(): P.op("dve", fn, r=[B_imp, B_cc, B_hb] + list(rr), w=[B_imp] + list(ww))
                    dq(lambda e, j=j: e.tensor_scalar(out=imp2[:], in0=J0[:], scalar1=CJ[:, j:j + 1], scalar2=None,
                                                      op0=ALU.subtract), ww=[B_imp2])
                    dq(lambda e: e.tensor_single_scalar(out=cb[:], in_=imp2[:], scalar=0.0, op=ALU.is_le),
                       rr=[B_imp2], ww=[B_cb])
                    dq(lambda e: e.scalar_tensor_tensor(out=imp2[:], in0=imp2[:], scalar=-1.0, in1=cb[:],
                                                        op0=ALU.is_ge, op1=ALU.mult), rr=[B_imp2, B_cb], ww=[B_imp2])
                    dq(lambda e: e.tensor_tensor(out=imp2[:], in0=imp2[:], in1=E0[:], op=ALU.max), rr=[B_imp2],
                       ww=[B_imp2])
                    dq(lambda e: e.scalar_tensor_tensor(out=imp[:], in0=imp2[:], scalar=1e4, in1=imp[:],
                                                        op0=ALU.mult, op1=ALU.add), rr=[B_imp2])
                    dq(lambda e: e.tensor_tensor(out=imp[:], in0=imp[:], in1=cb[:], op=ALU.mult), rr=[B_cb])
                    dq(lambda e: e.tensor_scalar(out=cb[:], in0=cb[:], scalar1=-1.0, scalar2=1e30, op0=ALU.add,
                                                 op1=ALU.mult), rr=[B_cb], ww=[B_cb])
                    dq(lambda e: e.tensor_tensor(out=imp[:], in0=imp[:], in1=cb[:], op=ALU.add), rr=[B_cb])
                    P.op("dve", lambda e: e.max(out=m8[:, 0:8], in_=imp[:]), r=[B_imp], w=[B_m8])
                    P.op("dve", lambda e: e.match_replace(out=imp2[:], in_to_replace=m8[:, 0:8], in_values=imp[:],
                                                          imm_value=-3e38), r=[B_imp, B_m8], w=[B_imp2])
                    P.op("dve", lambda e: e.max(out=m8[:, 8:16], in_=imp2[:]), r=[B_imp2], w=[B_m8])
                    P.op("dve", lambda e: e.tensor_scalar(out=selq[:], in0=imp[:], scalar1=m8[:, 15:16], scalar2=None,
                                                          op0=ALU.is_ge), r=[B_imp, B_m8], w=[B_selq])
                    tiles = []
                    for t in range(max(0, 2 * j - 4), 2 * j + 2):
                        r_ = t - (2 * j - 4)
                        tiles.append(dict(kT=KwT[:, t * 128:(t + 1) * 128], Bk=B_KwT, L=Lsel[:, 2 * j - t + 1, :],
                                          R=Rsel[g][:], V=Vw[:, t, :], Bv=B_Vw, wm=WM.get(r_)))
                    attn_branch(tiles, 2, g, ci, False)
                    P.op("pe", lambda e: e.transpose(out=PS[4][:].bitcast(BF16)[:, 0:128], in_=selq[:],
                                                     identity=ident_b[:]), r=[B_selq, B_const], w=[B_PS[4]])
                    for h in range(4):
                        P.op("dve", lambda e, h=h: e.tensor_scalar(out=selb[:, h, :], in0=PS[4][:].bitcast(BF16)[:, 0:128],
                                                                   scalar1=BIGM, scalar2=-BIGM, op0=ALU.mult,
                                                                   op1=ALU.add), r=[B_PS[4]], w=[B_selb])
                    tiles = []
                    for t in range(0, 2 * j + 2):
                        wm = WM[4] if t == 2 * j else (WM[5] if t == 2 * j + 1 else None)
                        tiles.append(dict(kT=KsT[:, t * 128:(t + 1) * 128], Bk=B_KsT, L=Lsel[:, 2 * j - t + 1, :],
                                          R=Rsel[g][:], V=Vs[:, t, :], Bv=B_Vs, wm=wm,
                                          sel=EXPM[:, t * 128:(t + 1) * 128]))
                    attn_branch(tiles, 1, g, ci, False)
                    P.op("act", lambda e, ci=ci: e.copy(out=ybf[ci][:], in_=yt[ci][:].rearrange("p h d -> p (h d)")),
                         r=[B_yt[ci]], w=[B_ybf[ci]])
                    P.dma("pool", MIXA[j * 128:(j + 1) * 128, g * 512:(g + 1) * 512], ybf[ci][:], r=[B_ybf[ci]],
                          w=[B_MIXA])

        P.barrier()
        if "E" in stages:
          with contextlib.ExitStack() as st_e:
            def sbe(name, shape, dt):
                return st_e.enter_context(nc.sbuf_tensor(name, list(shape), dt))

            TC2 = 512
            xe = sbe("xe", [128, 4, D], F32)
            B_xe = [Buf("xe%d" % i) for i in range(4)]
            mch = sbe("mch", [128, 4, D], BF16)
            B_mch = Buf("mch"); B_mch2 = Buf("mch2")
            og = [sbe("og%d" % i, [128, 1024], F32) for i in range(2)]; B_og = [Buf("og%d" % i) for i in range(2)]
            ogi = [0]
            mT = sbe("mT", [128, 16, TC2], BF16)
            B_mT = Buf("mT")
            AT = sbe("AT", [128, 64, TC2], BF16)
            B_AT = Buf("AT")
            wp = [sbe("wp%d" % i, [128, 16, 512], BF16) for i in range(2)]
            B_wp = [Buf("wp%d" % i) for i in range(2)]
            gmlp = sbe("gmlp", [128, D], F32)
            B_gmlp = Buf("gmlp")
            hbe = [sbe("hbe%d" % i, [128, D], BF16) for i in range(2)]
            B_hbe = [Buf("hbe%d" % i) for i in range(2)]
            junke = sbe("junke", [128, D], BF16)
            B_junke = Buf("junke")
            sse = [sbe("sse%d" % i, [128, 4], F32) for i in range(2)]
            B_sse = [Buf("sse%d" % i) for i in range(2)]
            sq = [sbe("sq%d" % i, [128, 512], F32) for i in range(2)]
            B_sq = [Buf("sq%d" % i) for i in range(2)]
            P.dma("sp", gmlp[:], bcast(norm_mlp_g[0:1, :]), w=[B_gmlp])
            P.dma("sp", gmix[:], bcast(norm_f_g[0:1, :]), w=[B_gmix])
            wpi = [0]
            hbi = [0]
            sqi = [0]
            me_cache = {}

            def me_of(e):
                if "v" not in me_cache:
                    me_cache["v"] = e.partition_id() % 2
                return me_cache["v"]

            def loadw(src_ap, rbufs):
                i = wpi[0] % 2
                wpi[0] += 1
                P.dma("sp", wp[i][:], src_ap, r=rbufs, w=[B_wp[i]])
                return i

            for ch in range(SO // TC2):
                base = ch * TC2
                P.dma("sp", xe[:], x_own[base:base + 512, :].rearrange("(t p) d -> p t d", p=128),
                      w=[B_xe[0], B_xe[1], B_xe[2], B_xe[3]])
                P.dma("sp", mch[:, :, 0:1024], MIXA[base:base + 512, :].rearrange("(t p) d -> p t d", p=128),
                      r=[B_MIXA], w=[B_mch])
                P.dma("pool", mch[:, :, 1024:2048],
                      lambda e, o=base: YMP[bass.ds(me_of(e) * SO + o, 512), :].rearrange("(t p) d -> p t d", p=128),
                      r=[B_YMP], w=[B_mch2])
                for tt in range(4):
                    oi = ogi[0] % 2
                    ogi[0] += 1
                    P.dma("sp", og[oi][:], OM[base + tt * 128:base + (tt + 1) * 128, :], r=[B_OM], w=[B_og[oi]])
                    P.op("act", lambda e, oi=oi: e.activation(out=og[oi][:], in_=og[oi][:], func=AF.Exp, scale=-1.0),
                         r=[B_og[oi]], w=[B_og[oi]])
                    P.op("pool", lambda e, oi=oi: e.tensor_scalar(out=og[oi][:], in0=og[oi][:], scalar1=1.0, scalar2=1.0,
                                                                  op0=ALU.add, op1=ALU.mult), r=[B_og[oi]], w=[B_og[oi]])
                    P.op("dve", lambda e, oi=oi: e.reciprocal(out=og[oi][:], in_=og[oi][:]), r=[B_og[oi]], w=[B_og[oi]])
                    P.op("dve", lambda e, oi=oi, tt=tt: e.tensor_tensor(out=mch[:, tt, 1024:2048], in0=mch[:, tt, 1024:2048],
                                                                        in1=og[oi][:], op=ALU.mult),
                         r=[B_og[oi], B_mch2], w=[B_mch2])
                    transpose_to(mch[:, tt, :], [B_mch, B_mch2],
                                 lambda f0, n, tt=tt: mT[:, f0:f0 + n, tt * 128:(tt + 1) * 128], B_mT)
                for cg in range(4):
                    wi = loadw(w_out_b[:, cg * 512:(cg + 1) * 512].rearrange("(f p) c -> p f c", p=128), [B_wob])
                    for tt in range(4):
                        def mm(e, wi=wi, tt=tt):
                            ins = None
                            for fc in range(16):
                                ins = e.matmul(PS[tt][:, :], lhsT=mT[:, fc, tt * 128:(tt + 1) * 128],
                                               rhs=wp[wi][:, fc, :], start=(fc == 0), stop=(fc == 15))
                            return ins
                        P.op("pe", mm, r=[B_wp[wi], B_mT], w=[B_PS[tt]])
                        P.op("dve", lambda e, tt=tt, cg=cg: e.tensor_tensor(
                            out=xe[:, tt, cg * 512:(cg + 1) * 512], in0=xe[:, tt, cg * 512:(cg + 1) * 512],
                            in1=PS[tt][:, :], op=ALU.add), r=[B_PS[tt], B_xe[tt]], w=[B_xe[tt]])
                for tt in range(4):
                    hi = hbi[0] % 2
                    hbi[0] += 1
                    rms_to_bf16(xe[:, tt, :], B_xe[tt], gmlp[:], B_gmlp, hbe[hi][:], B_hbe[hi], junke[:], B_junke,
                                sse[hi], B_sse[hi])
                    transpose_to(hbe[hi], B_hbe[hi],
                                 lambda f0, n, tt=tt: mT[:, f0:f0 + n, tt * 128:(tt + 1) * 128], B_mT)
                for jg in range(16):
                    wi = loadw(w1_b[:, jg * 512:(jg + 1) * 512].rearrange("(f p) c -> p f c", p=128), [B_w1b])
                    for jb in range(4):
                        j = jg * 4 + jb
                        bi = jb
                        def mm(e, wi=wi, jb=jb, bi=bi):
                            ins = None
                            for fc in range(16):
                                ins = e.matmul(PS[bi][:, :], lhsT=wp[wi][:, fc, jb * 128:(jb + 1) * 128],
                                               rhs=mT[:, fc, :], start=(fc == 0), stop=(fc == 15))
                            return ins
                        P.op("pe", mm, r=[B_wp[wi], B_mT], w=[B_PS[bi]])
                        qi = sqi[0] % 2
                        sqi[0] += 1
                        P.op("act", lambda e, qi=qi, bi=bi: e.activation(out=sq[qi][:], in_=PS[bi][:, :], func=AF.Square),
                             r=[B_PS[bi]], w=[B_sq[qi]])
                        P.op("dve", lambda e, qi=qi, bi=bi, j=j: e.scalar_tensor_tensor(
                            out=AT[:, j, :], in0=PS[bi][:, :], scalar=0.0, in1=sq[qi][:], op0=ALU.is_gt, op1=ALU.mult),
                            r=[B_PS[bi], B_sq[qi]], w=[B_AT])
                for cg in range(4):
                    for jq in range(4):
                        wi = loadw(w2_b[jq * 2048:(jq + 1) * 2048, cg * 512:(cg + 1) * 512].rearrange(
                            "(j p) c -> p j c", p=128), [B_w2b])
                        for tt in range(4):
                            def mm(e, wi=wi, tt=tt, jq=jq):
                                ins = None
                                for jj in range(16):
                                    ins = e.matmul(PS[tt][:, :], lhsT=AT[:, jq * 16 + jj, tt * 128:(tt + 1) * 128],
                                                   rhs=wp[wi][:, jj, :], start=(jq == 0 and jj == 0),
                                                   stop=(jq == 3 and jj == 15))
                                return ins
                            P.op("pe", mm, r=[B_wp[wi], B_AT], w=[B_PS[tt]])
                    for tt in range(4):
                        P.op("dve", lambda e, tt=tt, cg=cg: e.tensor_tensor(
                            out=xe[:, tt, cg * 512:(cg + 1) * 512], in0=xe[:, tt, cg * 512:(cg + 1) * 512],
                            in1=PS[tt][:, :], op=ALU.add), r=[B_PS[tt], B_xe[tt]], w=[B_xe[tt]])
                for tt in range(4):
                    hi = hbi[0] % 2
                    hbi[0] += 1
                    rms_to_bf16(xe[:, tt, :], B_xe[tt], gmix[:], B_gmix, xe[:, tt, :], B_xe[tt], junke[:], B_junke,
                                sse[hi], B_sse[hi])
                    P.dma("pool", y[base + tt * 128:base + (tt + 1) * 128, :], xe[:, tt, :], r=[B_xe[tt]], w=[B_Y],
                          is_out=True)
        P.finish()
    return nc


def make_in_map(inp, xb, h=0):
    f = lambda a: np.ascontiguousarray(np.asarray(a, dtype=np.float32))
    S_ = xb.shape[0]
    x_own = np.ascontiguousarray(np.asarray(xb, dtype=np.float32).reshape(S_ // 256, 2, 128, D)[:, h].reshape(S_ // 2, D))
    return {
        "x": f(xb),
        "x_own": x_own,
        "hsel": np.full((1, 1), float(h), dtype=np.float32),
        "norm_mix_g": f(inp["norm_mix_g"]).reshape(1, D),
        "w_in": f(inp["w_in"]).reshape(D, DIN),
        "w_cmp_k1": f(inp["w_cmp_k1"]).reshape(4096, 128),
        "w_cmp_k2": f(inp["w_cmp_k2"]).reshape(128, 128),
        "pos_cmp_k": f(inp["pos_cmp_k"]).reshape(32, 128),
        "w_cmp_v1": f(inp["w_cmp_v1"]).reshape(4096, 128),
        "w_cmp_v2": f(inp["w_cmp_v2"]).reshape(128, 128),
        "pos_cmp_v": f(inp["pos_cmp_v"]).reshape(32, 128),
        "conv_w": f(inp["conv_w"]).reshape(4, D),
        "conv_b": f(inp["conv_b"]).reshape(1, D),
        "b_igate": f(inp["b_igate"]).reshape(4, 1),
        "b_fgate": f(inp["b_fgate"]).reshape(4, 1),
        "mlstm_norm_g": f(inp["mlstm_norm_g"]).reshape(1, 1024),
        "w_out": f(inp["w_out"]).reshape(D, D),
        "norm_mlp_g": f(inp["norm_mlp_g"]).reshape(1, D),
        "w_mlp_in": f(inp["w_mlp_in"]).reshape(D, DFF),
        "w_mlp_out": f(inp["w_mlp_out"]).reshape(DFF, D),
        "norm_f_g": f(inp["norm_f_g"]).reshape(1, D),
    }


def kernel(**inputs):
    x = np.asarray(inputs["x"], dtype=np.float32)
    B, S, _ = x.shape
    nc = build(S)
    in_maps = []
    for c in range(2 * B):
        in_maps.append(make_in_map(inputs, x[c // 2], c % 2))
    res = run_bass_kernel_spmd(nc, in_maps, core_ids=list(range(2 * B)))
    out = np.empty((B, S, D), dtype=np.float32)
    ov = out.reshape(B, S // 256, 2, 128, D)
    for c in range(2 * B):
        b, h = c // 2, c % 2
        ov[b, :, h] = np.asarray(res.results[c]["y"], dtype=np.float32).reshape(S // 256, 128, D)
    return out
```

```python
import contextlib
import numpy as np
import concourse.bass as bass
import concourse.mybir as mybir
from concourse.bass_utils import run_bass_kernel_spmd

F32 = mybir.dt.float32
BF16 = mybir.dt.bfloat16
ALU = mybir.AluOpType
AF = mybir.ActivationFunctionType
AX = mybir.AxisListType

D = 2048
DIN = 6688
DFF = 8192
EPS = 1e-6
NEG = -1e30
BIGM = 32768.0
EPOCH = 30000


class Buf:
    __slots__ = ("name", "w", "r", "sem", "semv", "accum", "wset", "excl")

    def __init__(self, name, accum=False, excl=False):
        self.excl = excl
        self.name = name
        self.w = None
        self.r = []
        self.sem = None
        self.semv = 0
        self.accum = accum
        self.wset = {}


class Prog:
    ENG = ("pe", "act", "dve", "pool", "sp")

    def __init__(self, nc, stack):
        self.nc = nc
        self.stack = stack
        self.streams = {e: [] for e in self.ENG}
        self.cnt = {e: 0 for e in self.ENG}
        self.sems = []
        self.esem = {e: self._newsem("e_" + e) for e in self.ENG}
        self.seen = {e: {} for e in self.ENG}
        self.out_toks = []
        self.pending = {}

    def _newsem(self, name):
        s = self.stack.enter_context(self.nc.semaphore(name + "_%d" % len(self.sems)))
        self.sems.append(s)
        return len(self.sems) - 1

    def _need(self, eng, tok, waits):
        if tok is None:
            return
        si, v = tok
        if self.seen[eng].get(si, 0) >= v:
            return
        if waits.get(si, 0) < v:
            waits[si] = v

    def _deps(self, eng, r, w):
        waits = {}
        for b in r:
            self._need(eng, b.w, waits)
            for si, v in b.wset.items():
                self._need(eng, (si, v), waits)
        for b in w:
            if not b.accum:
                self._need(eng, b.w, waits)
            for t in b.r:
                self._need(eng, t, waits)
        if eng == "pe":
            waits.pop(self.esem["pe"], None)
        for si, v in waits.items():
            self.seen[eng][si] = v
        return list(waits.items())

    def op(self, eng, fn, r=(), w=()):
        if any(b.excl for b in r):
            w = list(w) + [b for b in r if b.excl and b not in w]
            r = [b for b in r if not b.excl]
        waits = self._deps(eng, r, w)
        if self.cnt[eng] >= EPOCH:
            self.esem[eng] = self._newsem("e_" + eng)
            self.cnt[eng] = 0
        self.cnt[eng] += 1
        tok = (self.esem[eng], self.cnt[eng])
        sems = self.sems

        def emit(e, waits=waits, fn=fn, tok=tok):
            for si, v in waits:
                e.wait_ge(sems[si], v)
            fn(e).then_inc(sems[tok[0]], 1)

        self.streams[eng].append(emit)
        for b in w:
            b.w = tok
            b.r = []
        for b in r:
            b.r.append(tok)
        return tok

    def dma(self, q, out, in_, r=(), w=(), is_out=False, **kw):
        dst = w[0]
        waits = self._deps(q, r, w)
        own = dst
        if dst.accum and len(r) > 0 and not r[0].accum:
            own = r[0]
        if own.sem is None:
            own.sem = self._newsem("d_" + own.name)
        own.semv += 16
        tok = (own.sem, own.semv)
        sems = self.sems

        def emit(e, waits=waits, tok=tok, out=out, in_=in_, kw=kw):
            for si, v in waits:
                e.wait_ge(sems[si], v)
            o_ap = out(e) if callable(out) else out
            i_ap = in_(e) if callable(in_) else in_
            e.dma_start(out=o_ap, in_=i_ap, **kw).then_inc(sems[tok[0]], 16)

        self.streams[q].append(emit)
        self.pending[tok[0]] = tok[1]
        for d_ in w:
            if d_.accum:
                d_.wset[tok[0]] = tok[1]
            else:
                d_.w = tok
                d_.r = []
        for b in r:
            b.r.append(tok)
        if is_out:
            self.out_toks.append(tok)
        return tok

    def barrier(self):
        toks = dict(self.pending)
        for e in self.ENG:
            if self.cnt[e] > 0:
                toks[self.esem[e]] = self.cnt[e]
        self.pending = {}
        sems = self.sems
        for eng in self.ENG:
            waits = [(si, v) for si, v in toks.items() if self.seen[eng].get(si, 0) < v
                     and not (eng != "sp" and si == self.esem[eng])]
            for si, v in waits:
                self.seen[eng][si] = v

            def emit(e, waits=waits):
                for si, v in waits:
                    e.wait_ge(sems[si], v)

            self.streams[eng].append(emit)

    def finish(self):
        last = {}
        for si, v in self.out_toks:
            last[si] = max(last.get(si, 0), v)
        sems = self.sems

        def fin(e):
            for si, v in last.items():
                e.wait_ge(sems[si], v)

        self.streams["pool"].append(fin)
        with self.nc.Block() as block:
            @block.tensor
            def _(e):
                for f in self.streams["pe"]:
                    f(e)

            @block.scalar
            def _(e):
                for f in self.streams["act"]:
                    f(e)

            @block.vector
            def _(e):
                for f in self.streams["dve"]:
                    f(e)

            @block.gpsimd
            def _(e):
                for f in self.streams["pool"]:
                    f(e)

            @block.sync
            def _(e):
                for f in self.streams["sp"]:
                    f(e)


_REGS = {}


def freg(e, val):
    key = (id(e), float(val))
    if key not in _REGS:
        _REGS[key] = e.to_reg(float(val))
    return _REGS[key]


class Rot:
    def __init__(self, items):
        self.items = items
        self.i = 0

    def next(self):
        it = self.items[self.i % len(self.items)]
        self.i += 1
        return it


def build(S, debug=False, stages="ABCDE"):
    _REGS.clear()
    NT = S // 128
    SO = S // 2
    NCMP = S // 16 - 1
    NCT = (NCMP + 127) // 128
    NCH = S // 64
    TCA = min(2048, S)
    nc = bass.Bass("TRN2", target_bir_lowering=False)

    def din(name, shape, dt=F32):
        return nc.dram_tensor(name, list(shape), dt, kind="ExternalInput").ap()

    x = din("x", [S, D])
    x_own = din("x_own", [S // 2, D])
    hsel = din("hsel", [1, 1])
    norm_mix_g = din("norm_mix_g", [1, D])
    w_in = din("w_in", [D, DIN])
    w_cmp_k1 = din("w_cmp_k1", [4096, 128])
    w_cmp_k2 = din("w_cmp_k2", [128, 128])
    pos_cmp_k = din("pos_cmp_k", [32, 128])
    w_cmp_v1 = din("w_cmp_v1", [4096, 128])
    w_cmp_v2 = din("w_cmp_v2", [128, 128])
    pos_cmp_v = din("pos_cmp_v", [32, 128])
    conv_w = din("conv_w", [4, D])
    conv_b = din("conv_b", [1, D])
    b_igate = din("b_igate", [4, 1])
    b_fgate = din("b_fgate", [4, 1])
    mlstm_norm_g = din("mlstm_norm_g", [1, 1024])
    w_out = din("w_out", [D, D])
    norm_mlp_g = din("norm_mlp_g", [1, D])
    w_mlp_in = din("w_mlp_in", [D, DFF])
    w_mlp_out = din("w_mlp_out", [DFF, D])
    norm_f_g = din("norm_f_g", [1, D])
    y = nc.dram_tensor("y", [SO, D], F32, kind="ExternalOutput").ap()

    def dscr(name, shape, dt, out=False):
        if out or (debug and name in debug):
            return nc.dram_tensor(name, list(shape), dt, kind="ExternalOutput").ap()
        return nc.dram_tensor(name, list(shape), dt).ap()

    w_in_b = dscr("w_in_b", [D, DIN], BF16)
    w_out_b = dscr("w_out_b", [D, D], BF16)
    w1_b = dscr("w1_b", [D, DFF], BF16)
    w2_b = dscr("w2_b", [DFF, D], BF16)
    QT = dscr("QT", [8, 128, S // 2], BF16)
    KCV = dscr("KCV", [4, 128, S], BF16)
    KST = dscr("KST", [2, 128, S], BF16)
    KWT = dscr("KWT", [2, 128, S], BF16)
    VS = dscr("VS", [S, 256], BF16)
    VWG = dscr("VWG", [S, 256], F32)
    GOWN = dscr("GOWN", [S // 2, 24], F32)
    QKM = dscr("QKM", [16, 128, S], BF16)
    QKC = dscr("QKC", [16, 128, S], BF16)
    VM = dscr("VM", [S, 1024], BF16)
    OM = dscr("OM", [S // 2, 1024], F32)
    IFT = dscr("IFT", [8, S], F32)
    MIXA = dscr("MIXA", [S // 2, 1024], BF16)
    YMP = dscr("YMP", [S, 1024], BF16)

    stack = contextlib.ExitStack()
    with stack:
        P = Prog(nc, stack)

        def sb(name, shape, dt):
            return stack.enter_context(nc.sbuf_tensor(name, list(shape), dt))

        def ps(name, shape, dt=F32):
            return stack.enter_context(nc.psum_tensor(name, list(shape), dt))

        B_wib = Buf("wib", accum=True)
        B_wob = Buf("wob", accum=True)
        B_w1b = Buf("w1b", accum=True)
        B_w2b = Buf("w2b", accum=True)
        B_QT = Buf("QT", True); B_KCV = Buf("KCV", True); B_KST = Buf("KST", True); B_KWT = Buf("KWT", True)
        B_VS = Buf("VS", True); B_VWG = Buf("VWG", True); B_QKM = Buf("QKM", True); B_QKC = Buf("QKC", True)
        B_VM = Buf("VM", True); B_OM = Buf("OM", True); B_IFT = Buf("IFT", True); B_MIX = Buf("MIX", True); B_MIXA = Buf("MIXA", True); B_YMP = Buf("YMP", True); B_GOWN = Buf("GOWN", True)
        B_Y = Buf("Y", True)

        def cast_w(src, dst, buf, rows, cols, rstep):
            for r0 in range(0, rows, rstep):
                P.dma("pool", dst[r0:r0 + rstep, :], src[r0:r0 + rstep, :], w=[buf])

        ident_b = sb("ident_b", [128, 128], BF16)
        ident_f = sb("ident_f", [128, 128], F32)
        ones_f = sb("ones_f", [128, 128], F32)
        eps_t = sb("eps_t", [128, 1], F32)
        one_t = sb("one_t", [128, 1], F32)
        B_const = Buf("const")

        P.op("pool", lambda e: e.memset(ones_f[:], 1.0), w=[B_const])
        P.op("pool", lambda e: e.memset(eps_t[:], EPS), w=[B_const])
        P.op("pool", lambda e: e.memset(one_t[:], 1.0), w=[B_const])
        P.op("pool", lambda e: e.affine_select(out=ident_f[:], in_=ones_f[:], pattern=[[-1, 128]],
                                               compare_op=ALU.is_equal, fill=freg(e, 0.0), base=0, channel_multiplier=1),
             r=[B_const], w=[B_const])
        P.op("pool", lambda e: e.tensor_copy(out=ident_b[:], in_=ident_f[:]), r=[B_const], w=[B_const])

        gmix = sb("gmix", [128, D], F32)
        B_gmix = Buf("gmix")
        def bcast(ap_row, n=128):
            return ap_row.partition_broadcast(n).rearrange("p o d -> p (o d)")

        P.dma("sp", gmix[:], bcast(norm_mix_g[0:1, :]), w=[B_gmix])

        WIN_GROUPS = [(0, 512), (512, 512), (1024, 512), (1536, 256), (2048, 256), (2584, 512), (3096, 512), (3608, 512),
                      (4120, 512), (6680, 8), (1792, 256), (2304, 256), (2560, 24), (4632, 512), (5144, 512), (5656, 512), (6168, 512)]
        B_wig = {}
        for (c0g, ncg) in WIN_GROUPS:
            B_wig[c0g] = Buf("wib%d" % c0g, accum=True)
            P.dma("pool", w_in_b[:, c0g:c0g + ncg], w_in[:, c0g:c0g + ncg], w=[B_wig[c0g]])

        PS = [ps("psb%d" % i, [128, 512], F32) for i in range(8)]
        B_PS = [Buf("ps%d" % i, excl=True) for i in range(8)]


        def rms_to_bf16(xt_ap, B_x, g_tile, B_g, hb_ap, B_hb, junk_ap, B_junk, ss_ap, B_ss):
            P.op("act", lambda e: e.activation(out=junk_ap, in_=xt_ap, func=AF.Square, accum_out=ss_ap[:, 0:1]),
                 r=[B_x], w=[B_junk, B_ss])
            P.op("act", lambda e: e.activation(out=ss_ap[:, 1:2], in_=ss_ap[:, 0:1], func=AF.Sqrt,
                                               bias=eps_t[:, 0:1], scale=1.0 / D),
                 r=[B_ss, B_const], w=[B_ss])
            P.op("dve", lambda e: e.reciprocal(out=ss_ap[:, 2:3], in_=ss_ap[:, 1:2]), r=[B_ss], w=[B_ss])
            P.op("dve", lambda e: e.scalar_tensor_tensor(out=hb_ap, in0=xt_ap, scalar=ss_ap[:, 2:3], in1=g_tile,
                                                         op0=ALU.mult, op1=ALU.mult),
                 r=[B_x, B_ss, B_g], w=[B_hb])

        tr_rot = [0]

        def transpose_to(hb_ap, B_hb, dst_fn, B_dst, nblk=16):
            for f0 in range(0, nblk, 4):
                bi = 6 + (tr_rot[0] % 2)
                tr_rot[0] += 1
                tp = PS[bi][:].bitcast(BF16)
                tpv = tp[:, 0:512].rearrange("p (a b) -> p a b", a=4)

                def tfn(e, f0=f0, tpv=tpv):
                    ins = None
                    for a in range(4):
                        ins = e.transpose(out=tpv[:, a, :], in_=hb_ap[:, (f0 + a) * 128:(f0 + a + 1) * 128],
                                          identity=ident_b[:])
                    return ins

                P.op("pe", tfn, r=(B_hb if isinstance(B_hb, list) else [B_hb]) + [B_const], w=[B_PS[bi]])
                eng = "act" if (tr_rot[0] % 2) else "dve"
                dst = dst_fn(f0, 4)
                if eng == "act":
                    P.op("act", lambda e, dst=dst, tpv=tpv: e.copy(out=dst, in_=tpv), r=[B_PS[bi]], w=[B_dst])
                else:
                    P.op("dve", lambda e, dst=dst, tpv=tpv: e.tensor_copy(out=dst, in_=tpv), r=[B_PS[bi]], w=[B_dst])

        FM_ALL = [
            (1024, 512, "kcv"), (1536, 256, "ks"), (2048, 256, "kw"),
            (2584, 512, "qkm0"), (3096, 512, "qkm1"), (3608, 512, "qkm2"), (4120, 512, "qkm3"), (6680, 8, "if"),
        ]
        TM_ALL = [(1792, 256, "vs"), (2304, 256, "vwg"), (4632, 512, "vm0"), (5144, 512, "vm1")]
        FM_OWN = [(0, 512, "q0"), (512, 512, "q1")]
        TM_OWN = [(2560, 24, "go"), (5656, 512, "om0"), (6168, 512, "om1")]
        with contextlib.ExitStack() as st_a:
            def sba(name, shape, dt):
                return st_a.enter_context(nc.sbuf_tensor(name, list(shape), dt))

            xa = [sba("xa%d" % i, [128, D], F32) for i in range(4)]
            B_xa = [Buf("xa%d" % i) for i in range(4)]
            hb = [sba("hb%d" % i, [128, D], BF16) for i in range(4)]
            B_hb = [Buf("hb%d" % i) for i in range(4)]
            junk = sba("junkA", [128, D], BF16)
            B_junk = Buf("junkA")
            ssA = [sba("ssA%d" % i, [128, 4], F32) for i in range(4)]
            B_ssA = [Buf("ssA%d" % i) for i in range(4)]
            hTs = [sba("hT%d" % i, [128, 16, 1024], BF16) for i in range(2)]
            B_hTs = [Buf("hT%d" % i) for i in range(2)]
            wg = [sba("wg%d" % i, [128, 16, 512], BF16) for i in range(2)]
            B_wg = [Buf("wg%d" % i) for i in range(2)]
            stg = [sba("stg%d" % i, [128, 2048], F32) for i in range(2)]
            B_stg = [Buf("stg%d" % i) for i in range(2)]
            ev = [0]
            wgi = [0]
            sti = [0]

            def evac(dst_ap, src_ap, B_src, B_dst, scale=None):
                ev[0] += 1
                if scale is not None:
                    P.op("act", lambda e: e.activation(out=dst_ap, in_=src_ap, func=AF.Copy, scale=scale),
                         r=[B_src], w=[B_dst])
                elif ev[0] % 2:
                    P.op("act", lambda e: e.copy(out=dst_ap, in_=src_ap), r=[B_src], w=[B_dst])
                else:
                    P.op("dve", lambda e: e.tensor_copy(out=dst_ap, in_=src_ap), r=[B_src], w=[B_dst])

            TCA = 1024
            xai = [0]
            bki = [0]

            def norm_tile(xsrc, c0, t, hT, B_hT):
                i = xai[0] % 4
                xai[0] += 1
                r0 = c0 + t * 128
                P.dma("sp", xa[i][:], xsrc[r0:r0 + 128, :], w=[B_xa[i]])
                rms_to_bf16(xa[i][:], B_xa[i], gmix[:], B_gmix, hb[i][:], B_hb[i], junk[:], B_junk,
                            ssA[i], B_ssA[i])
                transpose_to(hb[i], B_hb[i],
                             lambda f0, n, t=t: hT[:, f0:f0 + n, t * 128:(t + 1) * 128], B_hT)

            def proj_chunk(c0, FM_GROUPS, TM_GROUPS, hT, B_hT, after_group):
                if True:
                    for (col0, ncols, kind) in FM_GROUPS:
                        wi = wgi[0] % 2
                        wgi[0] += 1
                        P.dma("sp", wg[wi][:, :, 0:ncols],
                              w_in_b[:, col0:col0 + ncols].rearrange("(f p) c -> p f c", p=128),
                              r=[B_wig[col0]], w=[B_wg[wi]])
                        for m0 in range(0, ncols, 128):
                            m = min(128, ncols - m0)
                            si = sti[0] % 2
                            sti[0] += 1
                            is_f32 = (kind == "if")
                            sview = stg[si][:] if is_f32 else stg[si][:].bitcast(BF16)
                            bki[0] += 2
                            for s0 in range(0, TCA, 512):
                                bi = (bki[0] + s0 // 512) % 4

                                def mm(e, wi=wi, m0=m0, m=m, s0=s0, bi=bi):
                                    ins = None
                                    for fc in range(16):
                                        ins = e.matmul(PS[bi][0:m, :], lhsT=wg[wi][:, fc, m0:m0 + m],
                                                       rhs=hT[:, fc, s0:s0 + 512], start=(fc == 0), stop=(fc == 15))
                                    return ins

                                P.op("pe", mm, r=[B_wg[wi], B_hT], w=[B_PS[bi]])
                                evac(sview[0:m, s0:s0 + 512], PS[bi][0:m, :], B_PS[bi], B_stg[si],
                                     scale=(128 ** -0.5) if kind in ("q0", "q1") else None)
                            blk = (col0 + m0)
                            if kind in ("q0", "q1"):
                                dst, Bd = QT[blk // 128, :, c0:c0 + TCA], B_QT
                            elif kind == "kcv":
                                dst, Bd = KCV[(blk - 1024) // 128, :, c0:c0 + TCA], B_KCV
                            elif kind == "ks":
                                dst, Bd = KST[(blk - 1536) // 128, :, c0:c0 + TCA], B_KST
                            elif kind == "kw":
                                dst, Bd = KWT[(blk - 2048) // 128, :, c0:c0 + TCA], B_KWT
                            elif kind == "if":
                                dst, Bd = IFT[:, c0:c0 + TCA], B_IFT
                            else:
                                dst, Bd = QKM[(blk - 2584) // 128, :, c0:c0 + TCA], B_QKM
                            P.dma("pool", dst, sview[0:m, 0:TCA], r=[B_stg[si]], w=[Bd])
                        after_group()
                    for (col0, ncols, kind) in TM_GROUPS:
                        wi = wgi[0] % 2
                        wgi[0] += 1
                        P.dma("sp", wg[wi][:, :, 0:ncols],
                              w_in_b[:, col0:col0 + ncols].rearrange("(f p) c -> p f c", p=128),
                              r=[B_wig[col0]], w=[B_wg[wi]])
                        is_f32 = kind in ("vwg", "om0", "om1", "go")
                        ntile = TCA // 128
                        for t0 in range(0, ntile, 4):
                            si = sti[0] % 2
                            sti[0] += 1
                            sview = (stg[si][:] if is_f32 else stg[si][:].bitcast(BF16))[:, 0:4 * ncols].rearrange(
                                "p (t c) -> p t c", t=4)
                            for tt in range(4):
                                t = t0 + tt
                                bi = tt % 4

                                def mm(e, wi=wi, t=t, bi=bi, ncols=ncols):
                                    ins = None
                                    for fc in range(16):
                                        ins = e.matmul(PS[bi][:, 0:ncols], lhsT=hT[:, fc, t * 128:(t + 1) * 128],
                                                       rhs=wg[wi][:, fc, 0:ncols], start=(fc == 0), stop=(fc == 15))
                                    return ins

                                P.op("pe", mm, r=[B_wg[wi], B_hT], w=[B_PS[bi]])
                                evac(sview[:, tt, :], PS[bi][:, 0:ncols], B_PS[bi], B_stg[si])
                            r0 = c0 + t0 * 128
                            if kind == "vs":
                                dst, Bd = VS[r0:r0 + 512, :], B_VS
                            elif kind == "vwg":
                                dst, Bd = VWG[r0:r0 + 512, :], B_VWG
                            elif kind == "go":
                                dst, Bd = GOWN[r0:r0 + 512, :], B_GOWN
                            elif kind in ("vm0", "vm1"):
                                o = 0 if kind == "vm0" else 512
                                dst, Bd = VM[r0:r0 + 512, o:o + 512], B_VM
                            else:
                                o = 0 if kind == "om0" else 512
                                dst, Bd = OM[r0:r0 + 512, o:o + 512], B_OM
                            P.dma("pool", dst.rearrange("(t p) c -> p t c", p=128), sview, r=[B_stg[si]], w=[Bd])
                        after_group()

            chunks = [(x, c0, FM_ALL, TM_ALL) for c0 in range(0, S, TCA)] + \
                     [(x_own, c0, FM_OWN, TM_OWN) for c0 in range(0, SO, TCA)]
            if "A" not in stages:
                chunks = []
            for t in range(TCA // 128 if chunks else 0):
                norm_tile(chunks[0][0], chunks[0][1], t, hTs[0], B_hTs[0])
            for k, (xsrc, c0, FMG, TMG) in enumerate(chunks):
                pend = []
                if k + 1 < len(chunks):
                    nx = chunks[k + 1]
                    pend = [(nx[0], nx[1], t, hTs[(k + 1) % 2], B_hTs[(k + 1) % 2]) for t in range(TCA // 128)]
                ng = len(FMG) + len(TMG)
                per = -(-len(pend) // ng) if pend else 0

                def after_group(pend=pend, per=per):
                    for _ in range(per):
                        if pend:
                            norm_tile(*pend.pop(0))

                proj_chunk(c0, FMG, TMG, hTs[k % 2], B_hTs[k % 2], after_group)
                while pend:
                    norm_tile(*pend.pop(0))

        P.barrier()
        cast_w(w_out, w_out_b, B_wob, D, D, 256)
        cast_w(w_mlp_in, w1_b, B_w1b, D, DFF, 128)
        cast_w(w_mlp_out, w2_b, B_w2b, DFF, D, 512)

        if "D" in stages:
          with contextlib.ExitStack() as st_d0:
            def sbd0(name, shape, dt):
                return st_d0.enter_context(nc.sbuf_tensor(name, list(shape), dt))
            TP = min(2048, S)
            prm = sbd0("prm", [5, D], F32); B_prm = Buf("prm")
            CW = sbd0("CW", [128, 16, 5], F32); B_CW = Buf("CW")
            U = [sbd0("U%d" % i, [128, 3 + TP], BF16) for i in range(2)]; B_U = [Buf("U%d" % i) for i in range(2)]
            Yc = [sbd0("Yc%d" % i, [128, TP], F32) for i in range(2)]; B_Yc = [Buf("Yc%d" % i) for i in range(2)]
            Z = [sbd0("Z%d" % i, [128, TP], BF16) for i in range(2)]; B_Z = [Buf("Z%d" % i) for i in range(2)]
            P.dma("sp", prm[0:4, :], conv_w, w=[B_prm])
            P.dma("sp", prm[4:5, :], conv_b, w=[B_prm])
            for b in range(16):
                P.op("pe", lambda e, b=b: e.transpose(out=PS[5][:, 0:5], in_=prm[0:5, b * 128:(b + 1) * 128],
                                                      identity=ident_f[0:5, 0:5]), r=[B_prm, B_const], w=[B_PS[5]])
                P.op("dve", lambda e, b=b: e.tensor_copy(out=CW[:, b, :], in_=PS[5][:, 0:5]), r=[B_PS[5]], w=[B_CW])
            it = 0
            for b in range(16):
                for p0 in range(0, S, TP):
                    i = it % 2
                    it += 1
                    if p0 == 0:
                        P.op("dve", lambda e, i=i: e.memset(U[i][:, 0:3], 0.0), w=[B_U[i]])
                        P.dma("sp", U[i][:, 3:3 + TP], QKM[b, :, 0:TP], r=[B_QKM], w=[B_U[i]])
                    else:
                        P.dma("sp", U[i][:, 0:3 + TP], QKM[b, :, p0 - 3:p0 + TP], r=[B_QKM], w=[B_U[i]])
                    P.op("dve", lambda e, i=i, b=b: e.tensor_scalar(out=Yc[i][:], in0=U[i][:, 3:3 + TP],
                                                                    scalar1=CW[:, b, 3:4], scalar2=CW[:, b, 4:5],
                                                                    op0=ALU.mult, op1=ALU.add),
                         r=[B_U[i], B_CW], w=[B_Yc[i]])
                    for tap in range(3):
                        P.op("dve", lambda e, i=i, b=b, tap=tap: e.scalar_tensor_tensor(
                            out=Yc[i][:], in0=U[i][:, tap:tap + TP], scalar=CW[:, b, tap:tap + 1], in1=Yc[i][:],
                            op0=ALU.mult, op1=ALU.add), r=[B_U[i], B_CW, B_Yc[i]], w=[B_Yc[i]])
                    P.op("act", lambda e, i=i: e.activation(out=Z[i][:], in_=Yc[i][:], func=AF.Silu),
                         r=[B_Yc[i]], w=[B_Z[i]])
                    if b >= 8:
                        P.op("pool", lambda e, i=i: e.tensor_scalar(out=Z[i][:], in0=Z[i][:], scalar1=0.0625,
                                                                    scalar2=1.0, op0=ALU.mult, op1=ALU.mult),
                             r=[B_Z[i]], w=[B_Z[i]])
                    P.dma("pool", QKC[b, :, p0:p0 + TP], Z[i][:], r=[B_Z[i]], w=[B_QKC])

          P.barrier()
          with contextlib.ExitStack() as st_d:
            def sbd(name, shape, dt):
                return st_d.enter_context(nc.sbuf_tensor(name, list(shape), dt))
            PCH = 16
            PW = PCH * 64
            NPC = NCH // PCH
            gi = sbd("gi", [4, PCH, 64], F32); gf = sbd("gf", [4, PCH, 64], F32)
            nbt = sbd("nbt", [4, PCH, 64], F32)
            Ug = [sbd("Ug%d" % i, [4, PW], F32) for i in range(2)]
            CMg = [sbd("CMg%d" % i, [4, PW], F32) for i in range(2)]
            WIg = [sbd("WIg%d" % i, [4, PW], F32) for i in range(2)]
            WSg = [sbd("WSg%d" % i, [4, PW], F32) for i in range(2)]
            FLg = [sbd("FLg%d" % i, [4, PW], F32) for i in range(2)]
            DECB = [sbd("DECB%d" % i, [128, 4, PCH], F32) for i in range(2)]
            B_gp = [Buf("gp%d" % i) for i in range(2)]
            B_DECB = [Buf("DECB%d" % i) for i in range(2)]
            marr = sbd("marr", [4, NCH + 1], F32); ncml = sbd("ncml", [4, NCH], F32); decay = sbd("decay", [4, NCH], F32)
            bI = sbd("bI", [4, 1], F32); bF = sbd("bF", [4, 1], F32)
            tmpg = sbd("tmpg", [4, PCH, 64], F32)
            OH = sbd("OH", [4, 4, 128], F32)
            B_g = Buf("gates")
            B_gi = Buf("gi"); B_gf = Buf("gf")
            B_OH = Buf("OH")
            P.dma("sp", bI[:], b_igate, w=[B_g])
            P.dma("sp", bF[:], b_fgate, w=[B_g])
            P.op("dve", lambda e: e.tensor_scalar(out=bF[:], in0=bF[:], scalar1=-1.0, scalar2=None, op0=ALU.mult),
                 r=[B_g], w=[B_g])
            P.op("dve", lambda e: e.memset(marr[:, 0:1], 0.0), r=[B_g], w=[B_g])
            P.op("pool", lambda e: e.memset(OH[:], 1.0), w=[B_OH])
            P.op("pool", lambda e: e.affine_select(out=OH[:], in_=OH[:], pattern=[[-1, 4], [0, 128]],
                                                   compare_op=ALU.is_equal, fill=freg(e, 0.0), base=0, channel_multiplier=1),
                 r=[B_OH], w=[B_OH])

            def gates(pc):
                pi = pc % 2
                t0 = pc * PW
                U_, CM_, WI_, WS_, FL_ = Ug[pi], CMg[pi], WIg[pi], WSg[pi], FLg[pi]
                dv = lambda fn: P.op("dve", fn, r=[B_g, B_gp[pi]], w=[B_g, B_gp[pi]])
                ac = lambda fn: P.op("act", fn, r=[B_g, B_gp[pi], B_const], w=[B_g, B_gp[pi]])
                P.dma("sp", gi[:].rearrange("p a b -> p (a b)"), IFT[0:4, t0:t0 + PW], r=[B_IFT, B_g], w=[B_gi])
                P.dma("sp", gf[:].rearrange("p a b -> p (a b)"), IFT[4:8, t0:t0 + PW], r=[B_IFT, B_g], w=[B_gf])
                P.op("act", lambda e: e.activation(out=gf[:], in_=gf[:], func=AF.Exp, bias=bF[:, 0:1], scale=-1.0),
                     r=[B_gf, B_g], w=[B_gf])
                P.op("act", lambda e: e.activation(out=gf[:], in_=gf[:], func=AF.Ln, bias=one_t[0:4, 0:1], scale=1.0),
                     r=[B_gf, B_const], w=[B_gf])
                for n in range(PCH):
                    P.op("dve", lambda e, n=n: e.tensor_tensor_scan(out=nbt[:, n, :], data0=ones_f[0:4, 0:64],
                                                                    data1=gf[:, n, :], initial=0.0, op0=ALU.mult,
                                                                    op1=ALU.add), r=[B_gf, B_const, B_g], w=[B_g])
                P.op("dve", lambda e: e.scalar_tensor_tensor(
                    out=U_[:, 0:PW], in0=gi[:].rearrange("p a b -> p (a b)"), scalar=bI[:, 0:1],
                    in1=nbt[:].rearrange("p a b -> p (a b)"), op0=ALU.add, op1=ALU.add),
                    r=[B_gi, B_g, B_gp[pi]], w=[B_g, B_gp[pi]])
                for n in range(PCH):
                    cn = pc * PCH + n
                    c0_ = n * 64
                    dv(lambda e, c0_=c0_, cn=cn: e.tensor_tensor_scan(
                        out=CM_[:, c0_:c0_ + 64], data0=U_[:, c0_:c0_ + 64], data1=U_[:, c0_:c0_ + 64],
                        initial=marr[:, cn:cn + 1], op0=ALU.max, op1=ALU.max))
                    dv(lambda e, c0_=c0_, cn=cn, n=n: e.tensor_tensor(
                        out=marr[:, cn + 1:cn + 2], in0=CM_[:, c0_ + 63:c0_ + 64], in1=nbt[:, n, 63:64],
                        op=ALU.subtract))
                    ac(lambda e, c0_=c0_, cn=cn: e.activation(out=WI_[:, c0_:c0_ + 64], in_=CM_[:, c0_:c0_ + 64],
                                                              func=AF.Exp, bias=marr[:, cn:cn + 1], scale=-1.0))
                dv(lambda e: e.tensor_scalar(out=ncml[:, pc * PCH:(pc + 1) * PCH], in0=CM_[:, 63:PW:64],
                                             scalar1=-1.0, scalar2=None, op0=ALU.mult))
                for n in range(PCH):
                    cn = pc * PCH + n
                    c0_ = n * 64
                    ac(lambda e, c0_=c0_, cn=cn: e.activation(out=WS_[:, c0_:c0_ + 64], in_=U_[:, c0_:c0_ + 64],
                                                              func=AF.Exp, bias=ncml[:, cn:cn + 1], scale=1.0))
                dv(lambda e: e.tensor_tensor(out=decay[:, pc * PCH:(pc + 1) * PCH],
                                             in0=marr[:, pc * PCH:(pc + 1) * PCH],
                                             in1=ncml[:, pc * PCH:(pc + 1) * PCH], op=ALU.add))
                ac(lambda e: e.activation(out=decay[:, pc * PCH:(pc + 1) * PCH],
                                          in_=decay[:, pc * PCH:(pc + 1) * PCH], func=AF.Exp))
                dv(lambda e: e.tensor_tensor(out=tmpg[:].rearrange("p a b -> p (a b)"),
                                             in0=nbt[:].rearrange("p a b -> p (a b)"), in1=CM_[:, 0:PW],
                                             op=ALU.subtract))
                ac(lambda e: e.activation(out=FL_[:, 0:PW], in_=tmpg[:].rearrange("p a b -> p (a b)"), func=AF.Exp))
                dv(lambda e: e.tensor_scalar(out=CM_[:, 0:PW], in0=CM_[:, 0:PW], scalar1=-1.0, scalar2=None,
                                             op0=ALU.mult))
                for hd in range(4):
                    P.op("pe", lambda e, hd=hd: e.matmul(PS[5][:, 0:PCH], lhsT=OH[:, hd, :],
                                                         rhs=decay[:, pc * PCH:(pc + 1) * PCH], start=True, stop=True),
                         r=[B_OH, B_g], w=[B_PS[5]])
                    P.op("dve", lambda e, hd=hd: e.tensor_copy(out=DECB[pi][:, hd, :], in_=PS[5][:, 0:PCH]),
                         r=[B_PS[5]], w=[B_DECB[pi]])

            qk = [sbd("qk%d" % i, [128, 16, 64], BF16) for i in range(2)]; B_qk = [Buf("qk%d" % i) for i in range(2)]
            vt = [sbd("vt%d" % i, [64, 4, 257], BF16) for i in range(2)]; B_vt = [Buf("vt%d" % i) for i in range(2)]
            gml = sbd("gml", [64, 1024], F32); B_gml = Buf("gml")
            Cst = sbd("Cst", [128, 8, 256], F32); B_C = [Buf("Cst%d" % h) for h in range(4)]
            Cb = sbd("Cb", [128, 8, 256], BF16); B_Cb = [Buf("Cb%d" % h) for h in range(4)]
            nvec = sbd("nvec", [128, 8], F32); B_nv = Buf("nvec")
            nbv = sbd("nbv", [128, 8], BF16); B_nbv = Buf("nbv")
            GT = [sbd("GT%d" % i, [64, 12], F32) for i in range(2)]; B_GT = [Buf("GT%d" % i) for i in range(2)]
            Am = [sbd("Am%d" % i, [64, 4, 64], F32) for i in range(2)]; B_Am = [Buf("Am%d" % i) for i in range(2)]
            scT = [sbd("scT%d" % i, [64, 4, 64], BF16) for i in range(2)]; B_scT = [Buf("scT%d" % i) for i in range(2)]
            qs = [sbd("qs%d" % i, [128, 8, 64], BF16) for i in range(2)]; B_qs = [Buf("qs%d" % i) for i in range(2)]
            kw = [sbd("kw%d" % i, [64, 4, 256], BF16) for i in range(2)]; B_kw = [Buf("kw%d" % i) for i in range(2)]
            rd = [sbd("rd%d" % i, [64, 16], F32) for i in range(2)]; B_rd = [Buf("rd%d" % i) for i in range(2)]
            junkd = sbd("junkd", [64, 256], BF16); B_junkd = Buf("junkd")
            ym = [sbd("ym%d" % i, [64, 1024], BF16) for i in range(2)]; B_ym = [Buf("ym%d" % i) for i in range(2)]
            P.dma("sp", gml[:], bcast(mlstm_norm_g[0:1, :], 64), w=[B_gml])
            P.op("dve", lambda e: e.memset(Cst[:], 0.0), w=B_C)
            P.op("dve", lambda e: e.memset(Cb[:], 0.0), w=B_Cb)
            P.op("dve", lambda e: e.memset(nvec[:], 0.0), w=[B_nv])
            P.op("dve", lambda e: e.memset(nbv[:], 0.0), w=[B_nbv])
            for i in range(2):
                P.op("dve", lambda e, i=i: e.memset(vt[i][:, :, 256:257], 1.0), w=[B_vt[i]])
            PSk = PS[5][:].bitcast(BF16)

            def pre(n):
                i = n % 2
                c0_ = n * 64
                pi = (n // PCH) % 2
                lo = (n % PCH) * 64
                P.dma("sp", qk[i][:], QKC[:, :, c0_:c0_ + 64].rearrange("b d t -> d b t"), r=[B_QKC], w=[B_qk[i]])
                P.dma("sp", vt[i][:, :, 0:256], VM[c0_:c0_ + 64, :].rearrange("t (h e) -> t h e", h=4), r=[B_VM],
                      w=[B_vt[i]])
                def gtr(e):
                    ins = None
                    for q_, src in enumerate((Ug[pi], WSg[pi], FLg[pi])):
                        ins = e.transpose(out=PS[0][0:64, 256 + 4 * q_:260 + 4 * q_], in_=src[0:4, lo:lo + 64],
                                          identity=ident_f[0:4, 0:4])
                    return ins
                P.op("pe", gtr, r=[B_gp[pi], B_const], w=[B_PS[0]])
                P.op("act", lambda e: e.copy(out=GT[i][:], in_=PS[0][0:64, 256:268]), r=[B_PS[0]], w=[B_GT[i]])
                def ktr(e):
                    ins = None
                    for b8 in range(8):
                        ins = e.transpose(out=PSk[0:64, b8 * 128:(b8 + 1) * 128], in_=qk[i][:, 8 + b8, :],
                                          identity=ident_b[:])
                    return ins
                P.op("pe", ktr, r=[B_qk[i], B_const], w=[B_PS[5]])
                for hd in range(4):
                    P.op("dve", lambda e, hd=hd: e.tensor_scalar(out=kw[i][:, hd, :],
                                                                 in0=PSk[0:64, hd * 256:(hd + 1) * 256],
                                                                 scalar1=GT[i][:, 4 + hd:5 + hd], scalar2=None,
                                                                 op0=ALU.mult),
                         r=[B_PS[5], B_GT[i]], w=[B_kw[i]])
                def bmm(e):
                    ins = None
                    for hd in range(4):
                        ins = e.matmul(PS[1][0:64, hd * 64:(hd + 1) * 64], lhsT=OH[:, hd, 0:64],
                                       rhs=CMg[pi][0:4, lo:lo + 64], start=True, stop=True)
                    for hd in range(4):
                        ins = e.matmul(PS[1][:, 256 + hd * 64:256 + (hd + 1) * 64], lhsT=OH[:, hd, :],
                                       rhs=WIg[pi][0:4, lo:lo + 64], start=True, stop=True)
                    return ins
                P.op("pe", bmm, r=[B_OH, B_gp[pi]], w=[B_PS[1]])
                def smm(e):
                    ins = None
                    for hd in range(4):
                        for dc in range(2):
                            ins = e.matmul(PS[0][0:64, hd * 64:(hd + 1) * 64], lhsT=qk[i][:, 8 + 2 * hd + dc, :],
                                           rhs=qk[i][:, 2 * hd + dc, :], start=(hd == 0 and dc == 0),
                                           stop=(hd == 3 and dc == 1), skip_group_check=True)
                    return ins
                P.op("pe", smm, r=[B_qk[i]], w=[B_PS[0]])
                for hd in range(4):
                    P.op("act", lambda e, hd=hd: e.activation(out=Am[i][:, hd, :],
                                                              in_=PS[1][0:64, hd * 64:(hd + 1) * 64],
                                                              func=AF.Exp, bias=GT[i][:, hd:hd + 1], scale=1.0),
                         r=[B_PS[1], B_GT[i]], w=[B_Am[i]])
                P.op("pool", lambda e: e.affine_select(out=Am[i][:], in_=Am[i][:], pattern=[[0, 4], [1, 64]],
                                                       compare_op=ALU.is_ge, fill=freg(e, 0.0), base=0,
                                                       channel_multiplier=-1), r=[B_Am[i]], w=[B_Am[i]])
                P.op("dve", lambda e: e.tensor_tensor(out=scT[i][:].rearrange("p h j -> p (h j)"),
                                                      in0=Am[i][:].rearrange("p h j -> p (h j)"),
                                                      in1=PS[0][0:64, 0:256], op=ALU.mult),
                     r=[B_Am[i], B_PS[0]], w=[B_scT[i]])
                for dc in range(2):
                    P.op("dve", lambda e, dc=dc: e.tensor_tensor(
                        out=qs[i][:, dc:8:2, :], in0=qk[i][:, dc:8:2, :],
                        in1=PS[1][:, 256:512].rearrange("p (h j) -> p h j", h=4), op=ALU.mult),
                        r=[B_qk[i], B_PS[1]], w=[B_qs[i]])

            def main(n):
                i = n % 2
                c0_ = n * 64
                pi = (n // PCH) % 2
                nl = n % PCH
                for hd in range(4):
                    def hmm(e, hd=hd):
                        o_ = PS[2 + hd // 2][0:64, (hd % 2) * 256:(hd % 2) * 256 + 256]
                        ins = None
                        for dc in range(2):
                            ins = e.matmul(o_, lhsT=qs[i][:, 2 * hd + dc, :], rhs=Cb[:, 2 * hd + dc, :],
                                           start=(hd % 2 == 0 and dc == 0), stop=False, skip_group_check=True)
                        ins = e.matmul(o_, lhsT=scT[i][:, hd, :], rhs=vt[i][:, hd, 0:256], start=False,
                                       stop=(hd % 2 == 1), skip_group_check=True)
                        return ins
                    P.op("pe", hmm, r=[B_qs[i], B_Cb[hd], B_scT[i], B_vt[i]], w=[B_PS[2 + hd // 2]])

                def dmm(e):
                    ins = None
                    for hd in range(4):
                        o_ = PS[4][0:64, hd:hd + 1]
                        for dc in range(2):
                            ins = e.matmul(o_, lhsT=qs[i][:, 2 * hd + dc, :], rhs=nbv[:, 2 * hd + dc:2 * hd + dc + 1],
                                           start=(hd == 0 and dc == 0), stop=False, skip_group_check=True)
                        ins = e.matmul(o_, lhsT=scT[i][:, hd, :], rhs=vt[i][:, hd, 256:257], start=False,
                                       stop=(hd == 3), skip_group_check=True)
                    return ins
                P.op("pe", dmm, r=[B_qs[i], B_scT[i], B_vt[i], B_nbv], w=[B_PS[4]])
                R_ = rd[i]
                P.op("dve", lambda e: e.tensor_copy(out=R_[:, 4:8], in_=PS[4][0:64, 0:4]), r=[B_PS[4]], w=[B_rd[i]])
                for hd in range(4):
                    ub = 6 + hd % 2

                    def umm(e, hd=hd, ub=ub):
                        ins = None
                        for dc in range(2):
                            ins = e.matmul(PS[ub][:, dc * 256:(dc + 1) * 256], lhsT=kw[i][:, hd, dc * 128:(dc + 1) * 128],
                                           rhs=vt[i][:, hd, 0:256], start=True, stop=True)
                        return ins
                    P.op("pe", umm, r=[B_kw[i], B_vt[i]], w=[B_PS[ub]])
                    P.op("dve", lambda e, hd=hd, ub=ub: e.scalar_tensor_tensor(
                        out=Cst[:, 2 * hd:2 * hd + 2, :], in0=Cst[:, 2 * hd:2 * hd + 2, :],
                        scalar=DECB[pi][:, hd, nl:nl + 1], in1=PS[ub][:, :].rearrange("p (a b) -> p a b", a=2),
                        op0=ALU.mult, op1=ALU.add), r=[B_C[hd], B_DECB[pi], B_PS[ub]], w=[B_C[hd]])
                    P.op("act", lambda e, hd=hd: e.copy(out=Cb[:, 2 * hd:2 * hd + 2, :], in_=Cst[:, 2 * hd:2 * hd + 2, :]),
                         r=[B_C[hd]], w=[B_Cb[hd]])

                def nmm(e):
                    ins = None
                    for hd in range(4):
                        for dc in range(2):
                            ins = e.matmul(PS[4][:, 128 + 2 * hd + dc:129 + 2 * hd + dc],
                                           lhsT=kw[i][:, hd, dc * 128:(dc + 1) * 128], rhs=vt[i][:, hd, 256:257],
                                           start=True, stop=True)
                    return ins
                P.op("pe", nmm, r=[B_kw[i], B_vt[i]], w=[B_PS[4]])
                for hd in range(4):
                    P.op("dve", lambda e, hd=hd: e.scalar_tensor_tensor(
                        out=nvec[:, 2 * hd:2 * hd + 2], in0=nvec[:, 2 * hd:2 * hd + 2],
                        scalar=DECB[pi][:, hd, nl:nl + 1], in1=PS[4][:, 128 + 2 * hd:130 + 2 * hd],
                        op0=ALU.mult, op1=ALU.add), r=[B_nv, B_DECB[pi], B_PS[4]], w=[B_nv])
                P.op("pool", lambda e: e.tensor_copy(out=nbv[:], in_=nvec[:]), r=[B_nv], w=[B_nbv])
                P.op("dve", lambda e: e.scalar_tensor_tensor(out=R_[:, 0:4], in0=R_[:, 4:8], scalar=-1.0, in1=R_[:, 4:8],
                                                             op0=ALU.mult, op1=ALU.max), r=[B_rd[i]], w=[B_rd[i]])
                P.op("dve", lambda e: e.tensor_tensor(out=R_[:, 0:4], in0=R_[:, 0:4], in1=GT[i][:, 8:12], op=ALU.max),
                     r=[B_rd[i], B_GT[i]], w=[B_rd[i]])
                P.op("dve", lambda e: e.reciprocal(out=R_[:, 0:4], in_=R_[:, 0:4]), r=[B_rd[i]], w=[B_rd[i]])
                for hd in range(4):
                    P.op("act", lambda e, hd=hd: e.activation(
                        out=junkd[:], in_=PS[2 + hd // 2][0:64, (hd % 2) * 256:(hd % 2) * 256 + 256], func=AF.Square,
                        accum_out=R_[:, 4 + hd:5 + hd]), r=[B_PS[2 + hd // 2], B_rd[i]], w=[B_junkd, B_rd[i]])
                P.op("dve", lambda e: e.tensor_tensor(out=R_[:, 8:12], in0=R_[:, 0:4], in1=R_[:, 0:4], op=ALU.mult),
                     r=[B_rd[i]], w=[B_rd[i]])
                P.op("dve", lambda e: e.tensor_tensor(out=R_[:, 8:12], in0=R_[:, 8:12], in1=R_[:, 4:8], op=ALU.mult),
                     r=[B_rd[i]], w=[B_rd[i]])
                P.op("act", lambda e: e.activation(out=R_[:, 8:12], in_=R_[:, 8:12], func=AF.Sqrt, bias=eps_t[0:64, 0:1],
                                                   scale=1.0 / 256), r=[B_rd[i], B_const], w=[B_rd[i]])
                P.op("dve", lambda e: e.reciprocal(out=R_[:, 8:12], in_=R_[:, 8:12]), r=[B_rd[i]], w=[B_rd[i]])
                P.op("dve", lambda e: e.tensor_tensor(out=R_[:, 12:16], in0=R_[:, 8:12], in1=R_[:, 0:4], op=ALU.mult),
                     r=[B_rd[i]], w=[B_rd[i]])
                for hd in range(4):
                    P.op("dve", lambda e, hd=hd: e.scalar_tensor_tensor(
                        out=ym[i][:, hd * 256:(hd + 1) * 256],
                        in0=PS[2 + hd // 2][0:64, (hd % 2) * 256:(hd % 2) * 256 + 256], scalar=R_[:, 12 + hd:13 + hd],
                        in1=gml[:, hd * 256:(hd + 1) * 256], op0=ALU.mult, op1=ALU.mult),
                        r=[B_PS[2 + hd // 2], B_rd[i], B_gml], w=[B_ym[i]])
                row0 = ((n // 2) % 2) * SO + ((n // 2) // 2) * 128 + (n % 2) * 64
                P.dma("pool", YMP[row0:row0 + 64, :], ym[i][:], r=[B_ym[i]], w=[B_YMP])

            gates(0)
            pre(0)
            for n in range(NCH):
                if n % PCH == PCH // 2 and n // PCH + 1 < NPC:
                    gates(n // PCH + 1)
                if n + 1 < NCH:
                    pre(n + 1)
                main(n)
        P.barrier()
        if "C" in stages:
          with contextlib.ExitStack() as st_c:
            def sbc(name, shape, dt):
                return st_c.enter_context(nc.sbuf_tensor(name, list(shape), dt))

            ND = NT + 16 * (NCT - 1)
            EXPM = sbc("EXPM", [128, S], BF16)
            NLS = NT + 1
            Lsel = sbc("Lsel", [128, NLS, 128], BF16)
            Lcmp = sbc("Lcmp", [128, ND, 128], BF16)
            Rsel = [sbc("Rsel%d" % g, [128, 4, 128], BF16) for g in range(2)]
            Rcmp = [sbc("Rcmp%d" % g, [128, 4, 128], BF16) for g in range(2)]
            OV = sbc("OV", [128, NCT, 128], BF16)
            qrow = sbc("qrow", [128, 128], F32)
            B_cc = Buf("cconst")
            pl = lambda fn, **k: P.op("pool", fn, r=[B_cc], w=[B_cc])
            P.op("dve", lambda e: e.memset(EXPM[:], 1.0), r=[B_cc], w=[B_cc])
            pl(lambda e: e.affine_select(out=EXPM[:], in_=EXPM[:], pattern=[[1, S]], compare_op=ALU.is_ge, fill=freg(e, 0.0),
                                         base=0, channel_multiplier=-64))
            pl(lambda e: e.affine_select(out=EXPM[:], in_=EXPM[:], pattern=[[-1, S]], compare_op=ALU.is_ge, fill=freg(e, 0.0),
                                         base=63, channel_multiplier=64))
            P.op("dve", lambda e: e.memset(Lsel[:], 0.0), r=[B_cc], w=[B_cc])
            P.op("dve", lambda e: e.memset(Lcmp[:], 0.0), r=[B_cc], w=[B_cc])
            pl(lambda e: e.iota(Lsel[0:1, :, :], pattern=[[0, NLS], [1, 128]], base=0, channel_multiplier=0,
                                allow_small_or_imprecise_dtypes=True))
            pl(lambda e: e.memset(Lsel[32:33, :, :], 1.0))
            pl(lambda e: e.iota(Lsel[64:65, :, :], pattern=[[128, NLS], [0, 128]], base=-128, channel_multiplier=0,
                                allow_small_or_imprecise_dtypes=True))
            pl(lambda e: e.iota(Lcmp[0:1, :, :], pattern=[[0, ND], [16, 128]], base=0, channel_multiplier=0,
                                allow_small_or_imprecise_dtypes=True))
            pl(lambda e: e.memset(Lcmp[32:33, :, :], 1.0))
            pl(lambda e: e.iota(Lcmp[64:65, :, :], pattern=[[128, ND], [0, 128]], base=-128 * 16 * (NCT - 1),
                                channel_multiplier=0, allow_small_or_imprecise_dtypes=True))
            pl(lambda e: e.iota(qrow[:], pattern=[[1, 128]], base=0, channel_multiplier=0,
                                allow_small_or_imprecise_dtypes=True))
            hb = sbc("hb", [128, 6], F32)
            B_hb = Buf("hb")
            P.dma("sp", hb[:, 0:1], bcast(hsel[0:1, 0:1]), w=[B_hb])
            P.op("dve", lambda e: e.tensor_scalar(out=hb[:, 1:2], in0=hb[:, 0:1], scalar1=128.0, scalar2=None,
                                                  op0=ALU.mult), r=[B_hb], w=[B_hb])
            P.op("dve", lambda e: e.tensor_scalar(out=hb[:, 2:3], in0=hb[:, 0:1], scalar1=2.0, scalar2=None,
                                                  op0=ALU.mult), r=[B_hb], w=[B_hb])
            P.op("dve", lambda e: e.tensor_scalar(out=hb[:, 3:4], in0=hb[:, 0:1], scalar1=-1.0, scalar2=1.0,
                                                  op0=ALU.mult, op1=ALU.add), r=[B_hb], w=[B_hb])
            P.op("dve", lambda e: e.tensor_scalar(out=hb[:, 4:5], in0=hb[:, 0:1], scalar1=-BIGM, scalar2=None,
                                                  op0=ALU.mult), r=[B_hb], w=[B_hb])
            P.op("dve", lambda e: e.tensor_scalar(out=hb[:, 5:6], in0=hb[:, 3:4], scalar1=-BIGM, scalar2=None,
                                                  op0=ALU.mult), r=[B_hb], w=[B_hb])
            P.op("dve", lambda e: e.tensor_scalar(out=qrow[:], in0=qrow[:], scalar1=hb[:, 1:2], scalar2=None,
                                                  op0=ALU.add), r=[B_hb, B_cc], w=[B_cc])
            Mdiag = sbc("Mdiag", [128, 128], F32); Manti = sbc("Manti", [128, 128], F32)
            WM = {r_: sbc("WM%d" % r_, [128, 4, 128], BF16) for r_ in (0, 1, 4, 5)}
            P.op("dve", lambda e: e.memset(Mdiag[:], 0.0), r=[B_cc], w=[B_cc])
            P.op("dve", lambda e: e.memset(Manti[:], 0.0), r=[B_cc], w=[B_cc])
            pl(lambda e: e.affine_select(out=Mdiag[:], in_=Mdiag[:], pattern=[[1, 128]], compare_op=ALU.is_ge,
                                         fill=freg(e, -BIGM), base=0, channel_multiplier=-1))
            pl(lambda e: e.affine_select(out=Manti[:], in_=Manti[:], pattern=[[-1, 128]], compare_op=ALU.is_ge,
                                         fill=freg(e, -BIGM), base=-1, channel_multiplier=1))
            dvc = lambda fn: P.op("dve", fn, r=[B_cc, B_hb], w=[B_cc])
            for hh in range(4):
                dvc(lambda e, hh=hh: e.tensor_scalar(out=WM[0][:, hh, :], in0=Manti[:], scalar1=hb[:, 3:4],
                                                     scalar2=hb[:, 4:5], op0=ALU.mult, op1=ALU.add))
                dvc(lambda e, hh=hh: e.tensor_scalar(out=WM[1][:, hh, :], in0=Manti[:], scalar1=hb[:, 0:1],
                                                     scalar2=None, op0=ALU.mult))
                dvc(lambda e, hh=hh: e.tensor_scalar(out=WM[4][:, hh, :], in0=Mdiag[:], scalar1=hb[:, 3:4],
                                                     scalar2=None, op0=ALU.mult))
                dvc(lambda e, hh=hh: e.tensor_scalar(out=WM[5][:, hh, :], in0=Mdiag[:], scalar1=hb[:, 0:1],
                                                     scalar2=hb[:, 5:6], op0=ALU.mult, op1=ALU.add))
            for g in range(2):
                pl(lambda e, g=g: e.memset(Rsel[g][:], 0.0))
                pl(lambda e, g=g: e.memset(Rcmp[g][:], 0.0))
                for r in range(4):
                    sl = 2.0 ** (-(4 * g + r + 1))
                    for R_ in (Rsel[g], Rcmp[g]):
                        pl(lambda e, R_=R_, r=r, sl=sl: e.memset(R_[0:1, r, :], sl))
                        pl(lambda e, R_=R_, r=r, sl=sl: e.memset(R_[64:65, r, :], -sl))
                    pl(lambda e, g=g, r=r, sl=sl: e.tensor_scalar(out=Rsel[g][32:33, r, :], in0=qrow[32:33, :],
                                                                  scalar1=-sl, scalar2=None, op0=ALU.mult))
                    pl(lambda e, g=g, r=r, sl=sl: e.tensor_scalar(out=Rcmp[g][32:33, r, :], in0=qrow[32:33, :],
                                                                  scalar1=-31.0, scalar2=-sl, op0=ALU.add, op1=ALU.mult))
            P.op("dve", lambda e: e.memset(OV[:], 1.0), r=[B_cc], w=[B_cc])
            pl(lambda e: e.affine_select(out=OV[:], in_=OV[:], pattern=[[2048, NCT], [-64, 128]], compare_op=ALU.is_ge,
                                         fill=freg(e, 0.0), base=31, channel_multiplier=16))
            pl(lambda e: e.affine_select(out=OV[:], in_=OV[:], pattern=[[-2048, NCT], [64, 128]], compare_op=ALU.is_ge,
                                         fill=freg(e, 0.0), base=48, channel_multiplier=-16))
            NJ = NT // 2
            J0 = sbc("J0", [128, 128], F32); CJ = sbc("CJ", [128, NJ], F32); E0 = sbc("E0", [128, 128], F32)
            V0 = sbc("V0", [128, 128], F32); OFFT = sbc("OFFT", [128, NJ * NCT], F32)
            cb = sbc("cb", [128, 128], F32); B_cb = Buf("cb")
            cm01 = [sbc("cm01_%d" % i, [128, 128], F32) for i in range(2)]; B_cm01 = [Buf("cm01_%d" % i) for i in range(2)]
            cmw = [sbc("cmw%d" % i, [128, 4, 128], BF16) for i in range(2)]; B_cmw = [Buf("cmw%d" % i) for i in range(2)]
            cmw_i = [0]
            ia = dict(channel_multiplier=0, allow_small_or_imprecise_dtypes=True)
            pl(lambda e: e.iota(J0[:], pattern=[[1, 128]], base=0, **ia))
            pl(lambda e: e.tensor_scalar(out=J0[64:128, :], in0=J0[64:128, :], scalar1=-1.0, scalar2=1.0,
                                         op0=ALU.add, op1=ALU.mult))
            pl(lambda e: e.iota(CJ[:], pattern=[[4, NJ]], base=0, **ia))
            P.op("dve", lambda e: e.tensor_scalar(out=CJ[:], in0=CJ[:], scalar1=hb[:, 2:3], scalar2=None, op0=ALU.add),
                 r=[B_cc, B_hb], w=[B_cc])
            P.op("dve", lambda e: e.memset(E0[:], 0.0), r=[B_cc], w=[B_cc])
            P.op("dve", lambda e: e.memset(E0[:, 0:1], 1.0), r=[B_cc], w=[B_cc])
            pl(lambda e: e.iota(V0[:], pattern=[[1, 128]], base=0, channel_multiplier=-16,
                                allow_small_or_imprecise_dtypes=True))
            pl(lambda e: e.iota(OFFT[:], pattern=[[256, NJ], [-2048, NCT]], base=-31, **ia))
            P.op("dve", lambda e: e.tensor_scalar(out=OFFT[:], in0=OFFT[:], scalar1=hb[:, 1:2], scalar2=None,
                                                  op0=ALU.add), r=[B_cc, B_hb], w=[B_cc])

            KsT = sbc("KsT", [128, S], BF16); B_KsT = Buf("KsT")
            KwT = sbc("KwT", [128, S], BF16); B_KwT = Buf("KwT")
            Vs = sbc("Vs", [128, NT, 129], BF16); B_Vs = Buf("Vs")
            Vw = sbc("Vw", [128, NT, 129], BF16); B_Vw = Buf("Vw")
            KCt = sbc("KCt", [128, NCT * 128], BF16); B_KCt = Buf("KCt")
            VCt = sbc("VCt", [128, NCT, 129], BF16); B_VCt = Buf("VCt")
            kin = KsT; B_kin = B_KsT
            W1c = sbc("W1c", [128, 32, 128], BF16); B_W1c = Buf("W1c")
            W2c = sbc("W2c", [128, 128], BF16); B_W2c = Buf("W2c")
            posf = sbc("posf", [32, 128], F32); B_posf = Buf("posf")
            posT = sbc("posT", [128, 32], BF16); B_posT = Buf("posT")
            c1 = sbc("c1", [128, 1], F32); B_c1 = Buf("c1")
            hidt = sbc("hidt", [128, NCT * 128], BF16); B_hidt = Buf("hidt")
            qt = [sbc("qt%d" % i, [128, 4, 128], BF16) for i in range(2)]; B_qt = [Buf("qt%d" % i) for i in range(2)]
            gt = [sbc("gt%d" % i, [128, 24], F32) for i in range(2)]; B_gt = [Buf("gt%d" % i) for i in range(2)]
            yt = [sbc("yt%d" % i, [128, 4, 128], F32) for i in range(2)]; B_yt = [Buf("yt%d" % i) for i in range(2)]
            ybf = [sbc("ybf%d" % i, [128, 512], BF16) for i in range(2)]; B_ybf = [Buf("ybf%d" % i) for i in range(2)]
            PT = [sbc("PT%d" % i, [128, 4, 128], BF16) for i in range(4)]; B_PT = [Buf("PT%d" % i) for i in range(4)]
            dn = sbc("dn", [128, 12], F32); B_dn = Buf("dn")
            imp = sbc("imp", [128, 128], F32); B_imp = Buf("imp")
            imp2 = sbc("imp2", [128, 128], F32); B_imp2 = Buf("imp2")
            m8 = sbc("m8", [128, 16], F32); B_m8 = Buf("m8")
            selq = sbc("selq", [128, 128], BF16); B_selq = Buf("selq")
            selb = sbc("selb", [128, 4, 128], BF16); B_selb = Buf("selb")
            pti = [0]
            sbi = [0]
            obi = [0]

            P.op("dve", lambda e: e.memset(hidt[:], 0.0), w=[B_hidt])

            def attn_branch(tiles, br, g, ci, is_cmp):
                n = len(tiles)
                ob = [(2, 3), (6, 7)][obi[0] % 2]
                obi[0] += 1
                slots = []

                def emit_S(i):
                    t = tiles[i]
                    sbk = (0, 1, 5)[sbi[0] % 3]
                    sbi[0] += 1
                    cmi = t.get("cmask")
                    if cmi is not None:
                        mi = cmw_i[0] % 2
                        cmw_i[0] += 1
                        P.op("dve", lambda e, mi=mi, cmi=cmi: e.tensor_scalar(
                            out=cm01[mi][:], in0=V0[:], scalar1=OFFT[:, cmi:cmi + 1], scalar2=0.0, op0=ALU.add,
                            op1=ALU.is_ge), r=[B_cc], w=[B_cm01[mi]])
                        for hh in range(4):
                            P.op("dve", lambda e, hh=hh, mi=mi: e.tensor_scalar(
                                out=cmw[mi][:, hh, :], in0=cm01[mi][:], scalar1=BIGM, scalar2=-BIGM, op0=ALU.mult,
                                op1=ALU.add), r=[B_cm01[mi]], w=[B_cmw[mi]])
                        t = dict(t)
                        t["wm"] = cmw[mi]
                        t["Bwm"] = B_cmw[mi]

                    def f(e, t=t, sbk=sbk):
                        e.matmul(PS[sbk][:, :], lhsT=t["kT"], rhs=qt[ci][:].rearrange("p h q -> p (h q)"),
                                 start=True, stop=False)
                        has_sel = t.get("sel") is not None
                        has_wm = t.get("wm") is not None
                        ins = e.matmul(PS[sbk][:, :], lhsT=t["L"], rhs=t["R"].rearrange("p h q -> p (h q)"),
                                       start=False, stop=not (has_sel or has_wm))
                        if has_sel:
                            ins = e.matmul(PS[sbk][:, :], lhsT=t["sel"], rhs=selb[:].rearrange("p h q -> p (h q)"),
                                           start=False, stop=not has_wm)
                        if has_wm:
                            ins = e.matmul(PS[sbk][:, :], lhsT=ident_b[:], rhs=t["wm"][:].rearrange("p h q -> p (h q)"),
                                           start=False, stop=True)
                        return ins

                    rb = [t["Bk"], B_qt[ci], B_cc, B_const] + ([B_selb] if t.get("sel") is not None else []) + (
                        [t["Bwm"]] if t.get("Bwm") is not None else [])
                    P.op("pe", f, r=rb, w=[B_PS[sbk]])
                    return sbk

                def emit_exp(i, sbk):
                    k = pti[0] % 4
                    pti[0] += 1
                    P.op("act", lambda e, k=k, sbk=sbk: e.activation(out=PT[k][:].rearrange("p h q -> p (h q)"),
                                                                     in_=PS[sbk][:, :], func=AF.Exp),
                         r=[B_PS[sbk]], w=[B_PT[k]])
                    return k

                def emit_pv(i, k):
                    t = tiles[i]

                    def f(e, t=t, k=k, i=i):
                        ins = None
                        for h in range(4):
                            ins = e.matmul(PS[ob[h // 2]][:, (h % 2) * 129:(h % 2) * 129 + 129], lhsT=PT[k][:, h, :],
                                           rhs=t["V"], start=(i == 0 and h % 2 == 0), stop=(i == n - 1),
                                           skip_group_check=True)
                        if is_cmp:
                            for h in range(4):
                                ins = e.matmul(PS[4][:, h * 128:(h + 1) * 128], lhsT=PT[k][:, h, :], rhs=t["ov"],
                                               start=(i == 0 and h == 0), stop=(i == n - 1), skip_group_check=True)
                        return ins

                    wb = [B_PS[ob[0]], B_PS[ob[1]]] + ([B_PS[4]] if is_cmp else [])
                    P.op("pe", f, r=[B_PT[k], t["Bv"], B_cc], w=wb)

                sbq = [emit_S(i) for i in range(min(2, n))]
                for i in range(n):
                    if i + 2 < n:
                        sbq.append(emit_S(i + 2))
                    k = emit_exp(i, sbq[i])
                    emit_pv(i, k)
                o0 = 4 * br
                for bkk in range(2):
                    P.op("dve", lambda e, bkk=bkk: e.tensor_scalar(
                        out=dn[:, o0 + 2 * bkk:o0 + 2 * bkk + 2],
                        in0=PS[ob[bkk]][:, 0:258].rearrange("p (h c) -> p h c", h=2)[:, :, 128],
                        scalar1=1e-30, scalar2=None, op0=ALU.max), r=[B_PS[ob[bkk]]], w=[B_dn])
                P.op("dve", lambda e: e.reciprocal(out=dn[:, o0:o0 + 4], in_=dn[:, o0:o0 + 4]), r=[B_dn], w=[B_dn])
                if is_cmp:
                    for h in range(4):
                        if h == 0:
                            P.op("dve", lambda e: e.tensor_scalar(out=imp[:], in0=PS[4][:, 0:128], scalar1=dn[:, o0:o0 + 1],
                                                                  scalar2=None, op0=ALU.mult),
                                 r=[B_PS[4], B_dn], w=[B_imp])
                        else:
                            P.op("dve", lambda e, h=h: e.scalar_tensor_tensor(
                                out=imp[:], in0=PS[4][:, h * 128:(h + 1) * 128], scalar=dn[:, o0 + h:o0 + h + 1],
                                in1=imp[:], op0=ALU.mult, op1=ALU.add), r=[B_PS[4], B_dn, B_imp], w=[B_imp])
                gcol = 12 * g + br
                P.op("dve", lambda e: e.tensor_tensor(out=dn[:, o0:o0 + 4], in0=dn[:, o0:o0 + 4],
                                                      in1=gt[ci][:, gcol:gcol + 10:3], op=ALU.mult),
                     r=[B_dn, B_gt[ci]], w=[B_dn])
                for h in range(4):
                    src = PS[ob[h // 2]][:, (h % 2) * 129:(h % 2) * 129 + 128]
                    if br == 0:
                        P.op("dve", lambda e, h=h, src=src: e.tensor_scalar(
                            out=yt[ci][:, h, :], in0=src, scalar1=dn[:, o0 + h:o0 + h + 1], scalar2=None, op0=ALU.mult),
                            r=[B_PS[ob[h // 2]], B_dn], w=[B_yt[ci]])
                    else:
                        P.op("dve", lambda e, h=h, src=src: e.scalar_tensor_tensor(
                            out=yt[ci][:, h, :], in0=src, scalar=dn[:, o0 + h:o0 + h + 1], in1=yt[ci][:, h, :],
                            op0=ALU.mult, op1=ALU.add), r=[B_PS[ob[h // 2]], B_dn, B_yt[ci]], w=[B_yt[ci]])

            for g in range(2):
                for kv in range(2):
                    w1src = (w_cmp_k1, w_cmp_v1)[kv]
                    w2src = (w_cmp_k2, w_cmp_v2)[kv]
                    possrc = (pos_cmp_k, pos_cmp_v)[kv]
                    P.dma("pool", W1c[:], w1src.rearrange("(r d) h -> d r h", d=128), w=[B_W1c])
                    P.dma("pool", W2c[:], w2src, w=[B_W2c])
                    P.dma("sp", posf[:], possrc, w=[B_posf])
                    P.dma("sp", kin[:], KCV[2 * kv + g, :, :], r=[B_KCV], w=[B_kin])
                    P.op("pe", lambda e: e.transpose(out=PS[5][:, 0:32], in_=posf[:], identity=ident_f[0:32, 0:32]),
                         r=[B_posf, B_const], w=[B_PS[5]])
                    P.op("dve", lambda e: e.tensor_copy(out=posT[:], in_=PS[5][:, 0:32]), r=[B_PS[5]], w=[B_posT])

                    def c1mm(e):
                        ins = None
                        for r in range(32):
                            ins = e.matmul(PS[5][:, 64:65], lhsT=W1c[:, r, :], rhs=posT[:, r:r + 1], start=(r == 0),
                                           stop=(r == 31))
                        return ins
                    P.op("pe", c1mm, r=[B_W1c, B_posT], w=[B_PS[5]])
                    P.op("dve", lambda e: e.tensor_copy(out=c1[:], in_=PS[5][:, 64:65]), r=[B_PS[5]], w=[B_c1])

                    def hmm(e):
                        ins = None
                        for r in range(32):
                            ins = e.matmul(PS[0][:, 0:NCMP], lhsT=W1c[:, r, :],
                                           rhs=kin[:, r:r + 16 * (NCMP - 1) + 1:16], start=(r == 0), stop=(r == 31))
                        return ins
                    P.op("pe", hmm, r=[B_W1c, B_kin], w=[B_PS[0]])
                    P.op("act", lambda e: e.activation(out=hidt[:, 0:NCMP], in_=PS[0][:, 0:NCMP], func=AF.Silu,
                                                       bias=c1[:, 0:1]), r=[B_PS[0], B_c1], w=[B_hidt])
                    if kv == 0:
                        P.op("pe", lambda e: e.matmul(PS[1][:, 0:NCT * 128], lhsT=W2c[:], rhs=hidt[:], start=True,
                                                      stop=True), r=[B_W2c, B_hidt], w=[B_PS[1]])
                        P.op("dve", lambda e: e.tensor_copy(out=KCt[:], in_=PS[1][:, 0:NCT * 128]), r=[B_PS[1]],
                             w=[B_KCt])
                    else:
                        def vmm(e):
                            ins = None
                            for tn in range(NCT):
                                ins = e.matmul(PS[1][:, tn * 128:(tn + 1) * 128], lhsT=hidt[:, tn * 128:(tn + 1) * 128],
                                               rhs=W2c[:], start=True, stop=True)
                            return ins
                        P.op("pe", vmm, r=[B_W2c, B_hidt], w=[B_PS[1]])
                        P.op("dve", lambda e: e.tensor_copy(
                            out=VCt[:, :, 0:128], in_=PS[1][:, 0:NCT * 128].rearrange("p (t c) -> p t c", t=NCT)),
                            r=[B_PS[1]], w=[B_VCt])
                        P.op("dve", lambda e: e.memset(VCt[:, :, 128:129], 1.0), r=[B_VCt], w=[B_VCt])
                P.dma("sp", KsT[:], KST[g, :, :], r=[B_KST], w=[B_KsT])
                P.dma("sp", KwT[:], KWT[g, :, :], r=[B_KWT], w=[B_KwT])
                P.dma("sp", Vs[:, :, 0:128], VS[:, g * 128:(g + 1) * 128].rearrange("(t p) c -> p t c", p=128),
                      r=[B_VS], w=[B_Vs])
                P.op("dve", lambda e: e.memset(Vs[:, :, 128:129], 1.0), r=[B_Vs], w=[B_Vs])
                P.dma("pool", Vw[:, :, 0:128], VWG[:, g * 128:(g + 1) * 128].rearrange("(t p) c -> p t c", p=128),
                      r=[B_VWG], w=[B_Vw])
                P.op("dve", lambda e: e.memset(Vw[:, :, 128:129], 1.0), r=[B_Vw], w=[B_Vw])
                for j in range(NT // 2):
                    ci = j % 2
                    P.dma("sp", qt[ci][:], QT[4 * g:4 * g + 4, :, j * 128:(j + 1) * 128].rearrange("h d t -> d h t"),
                          r=[B_QT], w=[B_qt[ci]])
                    P.dma("sp", gt[ci][:], GOWN[j * 128:(j + 1) * 128, :], r=[B_GOWN], w=[B_gt[ci]])
                    P.op("act", lambda e, ci=ci: e.activation(out=gt[ci][:], in_=gt[ci][:], func=AF.Exp, scale=-1.0),
                         r=[B_gt[ci]], w=[B_gt[ci]])
                    P.op("dve", lambda e, ci=ci: e.tensor_scalar(out=gt[ci][:], in0=gt[ci][:], scalar1=1.0, scalar2=None,
                                                                 op0=ALU.add), r=[B_gt[ci]], w=[B_gt[ci]])
                    P.op("dve", lambda e, ci=ci: e.reciprocal(out=gt[ci][:], in_=gt[ci][:]), r=[B_gt[ci]], w=[B_gt[ci]])
                    tmax = min(NCT - 1, (8 * (2 * j + 1) + 6) // 128)
                    tiles = []
                    for tn in range(tmax + 1):
                        full = (128 * tn + 127) <= (16 * j - 2)
                        dl = 2 * j - 16 * tn + 16 * (NCT - 1)
                        tiles.append(dict(kT=KCt[:, tn * 128:(tn + 1) * 128], Bk=B_KCt, L=Lcmp[:, dl, :], R=Rcmp[g][:],
                                          V=VCt[:, tn, :], Bv=B_VCt, ov=OV[:, tn, :],
                                          cmask=None if full else (j * NCT + tn)))
                    attn_branch(tiles, 0, g, ci, True)
                    dq = lambda fn, rr=(), ww=(): P.op("dve", fn, r=[B_imp, B_cc, B_hb] + list(rr), w=[B_imp] + list(ww))
                    dq(lambda e, j=j: e.tensor_scalar(out=imp2[:], in0=J0[:], scalar1=CJ[:, j:j + 1], scalar2=None,
                                                      op0=ALU.subtract), ww=[B_imp2])
                    dq(lambda e: e.tensor_single_scalar(out=cb[:], in_=imp2[:], scalar=0.0, op=ALU.is_le),
                       rr=[B_imp2], ww=[B_cb])
                    dq(lambda e: e.scalar_tensor_tensor(out=imp2[:], in0=imp2[:], scalar=-1.0, in1=cb[:],
                                                        op0=ALU.is_ge, op1=ALU.mult), rr=[B_imp2, B_cb], ww=[B_imp2])
                    dq(lambda e: e.tensor_tensor(out=imp2[:], in0=imp2[:], in1=E0[:], op=ALU.max), rr=[B_imp2],
                       ww=[B_imp2])
                    dq(lambda e: e.scalar_tensor_tensor(out=imp[:], in0=imp2[:], scalar=1e4, in1=imp[:],
                                                        op0=ALU.mult, op1=ALU.add), rr=[B_imp2])
                    dq(lambda e: e.tensor_tensor(out=imp[:], in0=imp[:], in1=cb[:], op=ALU.mult), rr=[B_cb])
                    dq(lambda e: e.tensor_scalar(out=cb[:], in0=cb[:], scalar1=-1.0, scalar2=1e30, op0=ALU.add,
                                                 op1=ALU.mult), rr=[B_cb], ww=[B_cb])
                    dq(lambda e: e.tensor_tensor(out=imp[:], in0=imp[:], in1=cb[:], op=ALU.add), rr=[B_cb])
                    P.op("dve", lambda e: e.max(out=m8[:, 0:8], in_=imp[:]), r=[B_imp], w=[B_m8])
                    P.op("dve", lambda e: e.match_replace(out=imp2[:], in_to_replace=m8[:, 0:8], in_values=imp[:],
                                                          imm_value=-3e38), r=[B_imp, B_m8], w=[B_imp2])
                    P.op("dve", lambda e: e.max(out=m8[:, 8:16], in_=imp2[:]), r=[B_imp2], w=[B_m8])
                    P.op("dve", lambda e: e.tensor_scalar(out=selq[:], in0=imp[:], scalar1=m8[:, 15:16], scalar2=None,
                                                          op0=ALU.is_ge), r=[B_imp, B_m8], w=[B_selq])
                    tiles = []
                    for t in range(max(0, 2 * j - 4), 2 * j + 2):
                        r_ = t - (2 * j - 4)
                        tiles.append(dict(kT=KwT[:, t * 128:(t + 1) * 128], Bk=B_KwT, L=Lsel[:, 2 * j - t + 1, :],
                                          R=Rsel[g][:], V=Vw[:, t, :], Bv=B_Vw, wm=WM.get(r_)))
                    attn_branch(tiles, 2, g, ci, False)
                    P.op("pe", lambda e: e.transpose(out=PS[4][:].bitcast(BF16)[:, 0:128], in_=selq[:],
                                                     identity=ident_b[:]), r=[B_selq, B_const], w=[B_PS[4]])
                    for h in range(4):
                        P.op("dve", lambda e, h=h: e.tensor_scalar(out=selb[:, h, :], in0=PS[4][:].bitcast(BF16)[:, 0:128],
                                                                   scalar1=BIGM, scalar2=-BIGM, op0=ALU.mult,
                                                                   op1=ALU.add), r=[B_PS[4]], w=[B_selb])
                    tiles = []
                    for t in range(0, 2 * j + 2):
                        wm = WM[4] if t == 2 * j else (WM[5] if t == 2 * j + 1 else None)
                        tiles.append(dict(kT=KsT[:, t * 128:(t + 1) * 128], Bk=B_KsT, L=Lsel[:, 2 * j - t + 1, :],
                                          R=Rsel[g][:], V=Vs[:, t, :], Bv=B_Vs, wm=wm,
                                          sel=EXPM[:, t * 128:(t + 1) * 128]))
                    attn_branch(tiles, 1, g, ci, False)
                    P.op("act", lambda e, ci=ci: e.copy(out=ybf[ci][:], in_=yt[ci][:].rearrange("p h d -> p (h d)")),
                         r=[B_yt[ci]], w=[B_ybf[ci]])
                    P.dma("pool", MIXA[j * 128:(j + 1) * 128, g * 512:(g + 1) * 512], ybf[ci][:], r=[B_ybf[ci]],
                          w=[B_MIXA])

        P.barrier()
        if "E" in stages:
          with contextlib.ExitStack() as st_e:
            def sbe(name, shape, dt):
                return st_e.enter_context(nc.sbuf_tensor(name, list(shape), dt))

            TC2 = 512
            xe = sbe("xe", [128, 4, D], F32)
            B_xe = [Buf("xe%d" % i) for i in range(4)]
            mch = sbe("mch", [128, 4, D], BF16)
            B_mch = Buf("mch"); B_mch2 = Buf("mch2")
            og = [sbe("og%d" % i, [128, 1024], F32) for i in range(2)]; B_og = [Buf("og%d" % i) for i in range(2)]
            ogi = [0]
            mT = sbe("mT", [128, 16, TC2], BF16)
            B_mT = Buf("mT")
            AT = sbe("AT", [128, 64, TC2], BF16)
            B_AT = Buf("AT")
            wp = [sbe("wp%d" % i, [128, 16, 512], BF16) for i in range(2)]
            B_wp = [Buf("wp%d" % i) for i in range(2)]
            gmlp = sbe("gmlp", [128, D], F32)
            B_gmlp = Buf("gmlp")
            hbe = [sbe("hbe%d" % i, [128, D], BF16) for i in range(2)]
            B_hbe = [Buf("hbe%d" % i) for i in range(2)]
            junke = sbe("junke", [128, D], BF16)
            B_junke = Buf("junke")
            sse = [sbe("sse%d" % i, [128, 4], F32) for i in range(2)]
            B_sse = [Buf("sse%d" % i) for i in range(2)]
            sq = [sbe("sq%d" % i, [128, 512], F32) for i in range(2)]
            B_sq = [Buf("sq%d" % i) for i in range(2)]
            P.dma("sp", gmlp[:], bcast(norm_mlp_g[0:1, :]), w=[B_gmlp])
            P.dma("sp", gmix[:], bcast(norm_f_g[0:1, :]), w=[B_gmix])
            wpi = [0]
            hbi = [0]
            sqi = [0]
            me_cache = {}

            def me_of(e):
                if "v" not in me_cache:
                    me_cache["v"] = e.partition_id() % 2
                return me_cache["v"]

            def loadw(src_ap, rbufs):
                i = wpi[0] % 2
                wpi[0] += 1
                P.dma("sp", wp[i][:], src_ap, r=rbufs, w=[B_wp[i]])
                return i

            def prep_mix_steps(ch):
                base = ch * TC2
                steps = []

                def step(tt):
                    if tt == 0:
                        P.dma("sp", mch[:, :, 0:1024], MIXA[base:base + 512, :].rearrange("(t p) d -> p t d", p=128),
                              r=[B_MIXA], w=[B_mch])
                        P.dma("pool", mch[:, :, 1024:2048],
                              lambda e, o=base: YMP[bass.ds(me_of(e) * SO + o, 512), :].rearrange("(t p) d -> p t d", p=128),
                              r=[B_YMP], w=[B_mch2])
                    oi = ogi[0] % 2
                    ogi[0] += 1
                    P.dma("sp", og[oi][:], OM[base + tt * 128:base + (tt + 1) * 128, :], r=[B_OM], w=[B_og[oi]])
                    P.op("act", lambda e: e.activation(out=og[oi][:], in_=og[oi][:], func=AF.Exp, scale=-1.0),
                         r=[B_og[oi]], w=[B_og[oi]])
                    P.op("pool", lambda e: e.tensor_scalar(out=og[oi][:], in0=og[oi][:], scalar1=1.0, scalar2=1.0,
                                                           op0=ALU.add, op1=ALU.mult), r=[B_og[oi]], w=[B_og[oi]])
                    P.op("dve", lambda e: e.reciprocal(out=og[oi][:], in_=og[oi][:]), r=[B_og[oi]], w=[B_og[oi]])
                    P.op("dve", lambda e: e.tensor_tensor(out=mch[:, tt, 1024:2048], in0=mch[:, tt, 1024:2048],
                                                          in1=og[oi][:], op=ALU.mult),
                         r=[B_og[oi], B_mch2], w=[B_mch2])
                    transpose_to(mch[:, tt, :], [B_mch, B_mch2],
                                 lambda f0, n: mT[:, f0:f0 + n, tt * 128:(tt + 1) * 128], B_mT)

                return [lambda tt=tt: step(tt) for tt in range(4)]

            for ch in range(SO // TC2):
                base = ch * TC2
                P.dma("sp", xe[:], x_own[base:base + 512, :].rearrange("(t p) d -> p t d", p=128),
                      w=[B_xe[0], B_xe[1], B_xe[2], B_xe[3]])
                for st_ in (prep_mix_steps(0) if ch == 0 else []):
                    st_()
                for cg in range(4):
                    wi = loadw(w_out_b[:, cg * 512:(cg + 1) * 512].rearrange("(f p) c -> p f c", p=128), [B_wob])
                    for tt in range(4):
                        def mm(e, wi=wi, tt=tt):
                            ins = None
                            for fc in range(16):
                                ins = e.matmul(PS[tt][:, :], lhsT=mT[:, fc, tt * 128:(tt + 1) * 128],
                                               rhs=wp[wi][:, fc, :], start=(fc == 0), stop=(fc == 15))
                            return ins
                        P.op("pe", mm, r=[B_wp[wi], B_mT], w=[B_PS[tt]])
                        P.op("dve", lambda e, tt=tt, cg=cg: e.tensor_tensor(
                            out=xe[:, tt, cg * 512:(cg + 1) * 512], in0=xe[:, tt, cg * 512:(cg + 1) * 512],
                            in1=PS[tt][:, :], op=ALU.add), r=[B_PS[tt], B_xe[tt]], w=[B_xe[tt]])
                for tt in range(4):
                    hi = hbi[0] % 2
                    hbi[0] += 1
                    rms_to_bf16(xe[:, tt, :], B_xe[tt], gmlp[:], B_gmlp, hbe[hi][:], B_hbe[hi], junke[:], B_junke,
                                sse[hi], B_sse[hi])
                    transpose_to(hbe[hi], B_hbe[hi],
                                 lambda f0, n, tt=tt: mT[:, f0:f0 + n, tt * 128:(tt + 1) * 128], B_mT)
                for jg in range(16):
                    wi = loadw(w1_b[:, jg * 512:(jg + 1) * 512].rearrange("(f p) c -> p f c", p=128), [B_w1b])
                    for jb in range(4):
                        j = jg * 4 + jb
                        bi = jb
                        def mm(e, wi=wi, jb=jb, bi=bi):
                            ins = None
                            for fc in range(16):
                                ins = e.matmul(PS[bi][:, :], lhsT=wp[wi][:, fc, jb * 128:(jb + 1) * 128],
                                               rhs=mT[:, fc, :], start=(fc == 0), stop=(fc == 15))
                            return ins
                        P.op("pe", mm, r=[B_wp[wi], B_mT], w=[B_PS[bi]])
                        qi = sqi[0] % 2
                        sqi[0] += 1
                        P.op("act", lambda e, qi=qi, bi=bi: e.activation(out=sq[qi][:], in_=PS[bi][:, :], func=AF.Square),
                             r=[B_PS[bi]], w=[B_sq[qi]])
                        P.op("dve", lambda e, qi=qi, bi=bi, j=j: e.scalar_tensor_tensor(
                            out=AT[:, j, :], in0=PS[bi][:, :], scalar=0.0, in1=sq[qi][:], op0=ALU.is_gt, op1=ALU.mult),
                            r=[B_PS[bi], B_sq[qi]], w=[B_AT])
                nxt = prep_mix_steps(ch + 1) if ch + 1 < SO // TC2 else []
                for cg in range(4):
                    for jq in range(4):
                        if nxt and (cg * 4 + jq) % 4 == 1:
                            nxt.pop(0)()
                        wi = loadw(w2_b[jq * 2048:(jq + 1) * 2048, cg * 512:(cg + 1) * 512].rearrange(
                            "(j p) c -> p j c", p=128), [B_w2b])
                        for tt in range(4):
                            def mm(e, wi=wi, tt=tt, jq=jq):
                                ins = None
                                for jj in range(16):
                                    ins = e.matmul(PS[tt][:, :], lhsT=AT[:, jq * 16 + jj, tt * 128:(tt + 1) * 128],
                                                   rhs=wp[wi][:, jj, :], start=(jq == 0 and jj == 0),
                                                   stop=(jq == 3 and jj == 15))
                                return ins
                            P.op("pe", mm, r=[B_wp[wi], B_AT], w=[B_PS[tt]])
                    for tt in range(4):
                        P.op("dve", lambda e, tt=tt, cg=cg: e.tensor_tensor(
                            out=xe[:, tt, cg * 512:(cg + 1) * 512], in0=xe[:, tt, cg * 512:(cg + 1) * 512],
                            in1=PS[tt][:, :], op=ALU.add), r=[B_PS[tt], B_xe[tt]], w=[B_xe[tt]])
                for tt in range(4):
                    hi = hbi[0] % 2
                    hbi[0] += 1
                    rms_to_bf16(xe[:, tt, :], B_xe[tt], gmix[:], B_gmix, xe[:, tt, :], B_xe[tt], junke[:], B_junke,
                                sse[hi], B_sse[hi])
                    P.dma("pool", y[base + tt * 128:base + (tt + 1) * 128, :], xe[:, tt, :], r=[B_xe[tt]], w=[B_Y],
                          is_out=True)
        P.finish()
    return nc


def make_in_map(inp, xb, h=0):
    f = lambda a: np.ascontiguousarray(np.asarray(a, dtype=np.float32))
    S_ = xb.shape[0]
    x_own = np.ascontiguousarray(np.asarray(xb, dtype=np.float32).reshape(S_ // 256, 2, 128, D)[:, h].reshape(S_ // 2, D))
    return {
        "x": f(xb),
        "x_own": x_own,
        "hsel": np.full((1, 1), float(h), dtype=np.float32),
        "norm_mix_g": f(inp["norm_mix_g"]).reshape(1, D),
        "w_in": f(inp["w_in"]).reshape(D, DIN),
        "w_cmp_k1": f(inp["w_cmp_k1"]).reshape(4096, 128),
        "w_cmp_k2": f(inp["w_cmp_k2"]).reshape(128, 128),
        "pos_cmp_k": f(inp["pos_cmp_k"]).reshape(32, 128),
        "w_cmp_v1": f(inp["w_cmp_v1"]).reshape(4096, 128),
        "w_cmp_v2": f(inp["w_cmp_v2"]).reshape(128, 128),
        "pos_cmp_v": f(inp["pos_cmp_v"]).reshape(32, 128),
        "conv_w": f(inp["conv_w"]).reshape(4, D),
        "conv_b": f(inp["conv_b"]).reshape(1, D),
        "b_igate": f(inp["b_igate"]).reshape(4, 1),
        "b_fgate": f(inp["b_fgate"]).reshape(4, 1),
        "mlstm_norm_g": f(inp["mlstm_norm_g"]).reshape(1, 1024),
        "w_out": f(inp["w_out"]).reshape(D, D),
        "norm_mlp_g": f(inp["norm_mlp_g"]).reshape(1, D),
        "w_mlp_in": f(inp["w_mlp_in"]).reshape(D, DFF),
        "w_mlp_out": f(inp["w_mlp_out"]).reshape(DFF, D),
        "norm_f_g": f(inp["norm_f_g"]).reshape(1, D),
    }


def kernel(**inputs):
    x = np.asarray(inputs["x"], dtype=np.float32)
    B, S, _ = x.shape
    nc = build(S)
    in_maps = []
    for c in range(2 * B):
        in_maps.append(make_in_map(inputs, x[c // 2], c % 2))
    res = run_bass_kernel_spmd(nc, in_maps, core_ids=list(range(2 * B)))
    out = np.empty((B, S, D), dtype=np.float32)
    ov = out.reshape(B, S // 256, 2, 128, D)
    for c in range(2 * B):
        b, h = c // 2, c % 2
        ov[b, :, h] = np.asarray(res.results[c]["y"], dtype=np.float32).reshape(S // 256, 128, D)
    return out
```
